# Optimizing a Trainium2 kernel written in Bass

```python
import jax, jax.numpy as jnp
from jax import lax
import numpy as np

D_MODEL = 1024
BATCH = 1
SEQ = 16384
DEPTH = 1
DEC_BATCH = 128
DEC_SEQ = 1
PAST_LEN = 16384
PAGE_SIZE = 128

D_MIX = D_MODEL
D_ATTN = D_MIX // 2
D_POOL = D_MIX - D_ATTN
HEAD_DIM = 64
N_HEADS = D_ATTN // HEAD_DIM
N_KV_HEADS = 2
GROUP = N_HEADS // N_KV_HEADS
D_KV = N_KV_HEADS * HEAD_DIM
WINDOW = 128
BLOCK = 128
POOL_WINDOWS = (2, 4, 8, 16)
N_POOL_GROUPS = len(POOL_WINDOWS)
POOL_GROUP_DIM = D_POOL // N_POOL_GROUPS
POOL_STATE = max(POOL_WINDOWS) - 1
D_IN_PROJ = D_ATTN + 2 * D_KV + D_ATTN + D_POOL + D_POOL
EPS = 1e-6
NEG_INF = -1e30

kernel_name = "hymba_swa_sink_pool_hybrid_step"


def _rms(x, w):
    xf = x.astype(jnp.float32)
    r = xf * lax.rsqrt(jnp.mean(xf * xf, axis=-1, keepdims=True) + EPS)
    return (r * w.astype(jnp.float32)).astype(x.dtype)


def _project(x, norm_w, w_in, q_norm_w, k_norm_w):
    B, T = x.shape[0], x.shape[1]
    z = _rms(x, norm_w) @ w_in
    o1 = D_ATTN
    o2 = o1 + D_KV
    o3 = o2 + D_KV
    o4 = o3 + D_ATTN
    o5 = o4 + D_POOL
    q = _rms(z[..., :o1].reshape(B, T, N_KV_HEADS, GROUP, HEAD_DIM), q_norm_w)
    k = _rms(z[..., o1:o2].reshape(B, T, N_KV_HEADS, HEAD_DIM), k_norm_w)
    v = z[..., o2:o3].reshape(B, T, N_KV_HEADS, HEAD_DIM)
    g_a = z[..., o3:o4]
    u = z[..., o4:o5]
    g_p = z[..., o5:]
    return q, k, v, g_a, u, g_p


def _sink_attn(q, k, v, q_pos, k_pos, sinks):
    s = jnp.einsum('bqkgd,bskd->bkgqs', q, k).astype(jnp.float32) * (HEAD_DIM ** -0.5)
    rel = q_pos[:, None] - k_pos[None, :]
    valid = (rel >= 0) & (rel <= WINDOW) & (k_pos[None, :] >= 0)
    s = jnp.where(valid, s, NEG_INF)
    sink = jnp.broadcast_to(sinks.astype(jnp.float32).reshape(N_KV_HEADS, GROUP, 1, 1),
                            s.shape[:-1] + (1,))
    p = jax.nn.softmax(jnp.concatenate([s, sink], axis=-1), axis=-1)[..., :-1]
    return jnp.einsum('bkgqs,bskd->bqkgd', p.astype(v.dtype), v)


def _attn_prompt(q, k, v, sinks):
    B, T = q.shape[0], q.shape[1]
    nb = T // BLOCK
    qb = q.reshape(B, nb, BLOCK, N_KV_HEADS, GROUP, HEAD_DIM)
    kb = k.reshape(B, nb, BLOCK, N_KV_HEADS, HEAD_DIM)
    vb = v.reshape(B, nb, BLOCK, N_KV_HEADS, HEAD_DIM)
    pad = ((0, 0), (1, 0), (0, 0), (0, 0), (0, 0))
    kk = jnp.concatenate([jnp.pad(kb, pad)[:, :-1], kb], axis=2)
    vv = jnp.concatenate([jnp.pad(vb, pad)[:, :-1], vb], axis=2)
    start = jnp.arange(nb, dtype=jnp.int32) * BLOCK
    q_pos = start[:, None] + jnp.arange(BLOCK, dtype=jnp.int32)[None, :]
    k_pos = start[:, None] - BLOCK + jnp.arange(2 * BLOCK, dtype=jnp.int32)[None, :]
    o = jax.vmap(_sink_attn, in_axes=(1, 1, 1, 0, 0, None), out_axes=1)(qb, kk, vv, q_pos, k_pos, sinks)
    return o.reshape(B, T, D_ATTN)


def _attn_sample(q, k, v, cache_k, cache_v, sinks):
    B, T = q.shape[0], q.shape[1]
    wc = cache_k.shape[1]
    kk = jnp.concatenate([cache_k.astype(k.dtype), k], axis=1)
    vv = jnp.concatenate([cache_v.astype(v.dtype), v], axis=1)
    q_pos = PAST_LEN + jnp.arange(T, dtype=jnp.int32)
    k_pos = PAST_LEN - wc + jnp.arange(wc + T, dtype=jnp.int32)
    o = _sink_attn(q, kk, vv, q_pos, k_pos, sinks)
    return o.reshape(B, T, D_ATTN), kk[:, -wc:], vv[:, -wc:]


def _pool(u_ext, pos0, w_pool, pool_scale):
    B = u_ext.shape[0]
    P = POOL_STATE
    T = u_ext.shape[1] - P
    uf = u_ext.astype(jnp.float32)
    cs = jnp.concatenate([jnp.zeros((B, 1, D_POOL), jnp.float32), jnp.cumsum(uf, axis=1)], axis=1)
    pos = pos0 + jnp.arange(T, dtype=jnp.int32)
    u_tok = uf[:, P:]
    outs = []
    for gi, w in enumerate(POOL_WINDOWS):
        lo, hi = gi * POOL_GROUP_DIM, (gi + 1) * POOL_GROUP_DIM
        win = cs[:, P + 1:P + 1 + T, lo:hi] - cs[:, P + 1 - w:P + 1 - w + T, lo:hi]
        cnt = jnp.minimum(w, pos + 1).astype(jnp.float32)[None, :, None]
        outs.append(win / cnt - u_tok[..., lo:hi])
    d = jnp.stack(outs, axis=2)
    y = jnp.einsum('btgc,gcd->btgd', d, w_pool.astype(jnp.float32)).reshape(B, T, D_POOL)
    return (y * pool_scale.astype(jnp.float32)).astype(u_ext.dtype)


def _merge(x, attn_o, g_a, pool_o, g_p, w_out):
    mixed = jnp.concatenate([attn_o * jax.nn.silu(g_a), pool_o * jax.nn.silu(g_p)], axis=-1)
    return x + mixed @ w_out


def setup_inputs(seed: int = 0) -> dict:
    key = jax.random.key(seed)
    ks = jax.random.split(key, 16)
    f32 = jnp.float32
    return {
        "x_prompt": jax.random.normal(ks[0], (BATCH, SEQ, D_MODEL), f32),
        "x_sample": jax.random.normal(ks[1], (DEC_BATCH, DEC_SEQ, D_MODEL), f32),
        "cache_k": jax.random.normal(ks[2], (DEPTH, DEC_BATCH, WINDOW, N_KV_HEADS, HEAD_DIM), f32),
        "cache_v": jax.random.normal(ks[3], (DEPTH, DEC_BATCH, WINDOW, N_KV_HEADS, HEAD_DIM), f32),
        "state_pool": jax.random.normal(ks[4], (DEPTH, DEC_BATCH, POOL_STATE, D_POOL), f32),
        "norm_w": 1.0 + 0.02 * jax.random.normal(ks[5], (DEPTH, D_MODEL), f32),
        "w_in": jax.random.normal(ks[6], (DEPTH, D_MODEL, D_IN_PROJ), f32) * D_MODEL ** -0.5,
        "q_norm_w": 1.0 + 0.02 * jax.random.normal(ks[7], (DEPTH, HEAD_DIM), f32),
        "k_norm_w": 1.0 + 0.02 * jax.random.normal(ks[8], (DEPTH, HEAD_DIM), f32),
        "sinks": 0.5 * jax.random.normal(ks[9], (DEPTH, N_HEADS), f32),
        "w_pool": jax.random.normal(ks[10], (DEPTH, N_POOL_GROUPS, POOL_GROUP_DIM, POOL_GROUP_DIM), f32) * POOL_GROUP_DIM ** -0.5,
        "pool_scale": 1.0 + 0.02 * jax.random.normal(ks[11], (DEPTH, D_POOL), f32),
        "w_out": jax.random.normal(ks[12], (DEPTH, D_MIX, D_MODEL), f32) * D_MIX ** -0.5,
    }


def reference(x_prompt, x_sample, cache_k, cache_v, state_pool, norm_w, w_in, q_norm_w,
              k_norm_w, sinks, w_pool, pool_scale, w_out):
    hp, hs = x_prompt, x_sample
    kp, vp, pp, kq, vq, pq = [], [], [], [], [], []
    for l in range(DEPTH):
        q, k, v, g_a, u, g_p = _project(hp, norm_w[l], w_in[l], q_norm_w[l], k_norm_w[l])
        a_o = _attn_prompt(q, k, v, sinks[l])
        u_ext = jnp.pad(u, ((0, 0), (POOL_STATE, 0), (0, 0)))
        p_o = _pool(u_ext, 0, w_pool[l], pool_scale[l])
        kp.append(k[:, -WINDOW:])
        vp.append(v[:, -WINDOW:])
        pp.append(u_ext[:, -POOL_STATE:])
        hp = _merge(hp, a_o, g_a, p_o, g_p, w_out[l])
        q, k, v, g_a, u, g_p = _project(hs, norm_w[l], w_in[l], q_norm_w[l], k_norm_w[l])
        a_o, k_new, v_new = _attn_sample(q, k, v, cache_k[l], cache_v[l], sinks[l])
        u_ext = jnp.concatenate([state_pool[l].astype(u.dtype), u], axis=1)
        p_o = _pool(u_ext, PAST_LEN, w_pool[l], pool_scale[l])
        kq.append(k_new)
        vq.append(v_new)
        pq.append(u_ext[:, -POOL_STATE:])
        hs = _merge(hs, a_o, g_a, p_o, g_p, w_out[l])
    return (hp, hs, jnp.stack(kp), jnp.stack(vp), jnp.stack(pp), jnp.stack(kq), jnp.stack(vq), jnp.stack(pq))
```

```python
import os
import numpy as np
from contextlib import ExitStack

import concourse.bass as bass
import concourse.mybir as mybir
from concourse.bass_utils import run_bass_kernel_spmd

F32 = mybir.dt.float32
BF16 = mybir.dt.bfloat16
AF = mybir.ActivationFunctionType
ALU = mybir.AluOpType
AX = mybir.AxisListType

NCORE = 8
DM = 1024
SEQ = 16384
TOK = SEQ // NCORE
TT = 512
NTILE = TOK // TT
NSMP = 128 // NCORE
NCOL = 2304
EPS = 1e-6
POOLW = (2, 4, 8, 16)

C_ID, C_MP, C_MC, C_MF, C_BLK = 0, 128, 256, 384, 512
NCB = 640
C_QW, C_KW, C_NW, C_PS, C_SK, C_RC, C_DM, C_EPS, C_M1 = 640, 641, 642, 650, 654, 658, 722, 786, 787
NCST = 788


class _Op:
    __slots__ = ("eng", "fn", "r", "w", "dma", "deps", "waits", "inc", "val", "seq", "c", "tbl", "t0", "t1", "nb", "grp")

    def __init__(self, eng, fn, r, w, dma, c, tbl, nb=0):
        self.eng, self.fn, self.r, self.w, self.dma = eng, fn, r, w, dma
        self.c, self.tbl, self.nb = c, tbl, nb
        self.deps = ()
        self.waits = []
        self.inc = False
        self.val = 0
        self.seq = 0
        self.t0 = self.t1 = 0.0


class Sched:
    ENG = ("pe", "act", "dve", "pool", "sp")
    DEFC = {"pe": 0.27, "act": 0.7, "dve": 0.6, "pool": 1.0, "sp": 3.0}
    ISSUE = {"sp": 0.08, "pool": 0.6, "act": 0.1}
    TBL_SWITCH = 1.4
    DMA_BW = 330e3
    XLAT = 0.15
    GRP_SWITCH = 0.1

    def __init__(self, nc):
        self.nc = nc
        self.ops = []
        self.grp = 0

    def add(self, eng, fn, r=(), w=(), dma=None, c=None, tbl=0, nb=0, pm=0):
        if c is None:
            c = self.DEFC["sp"] if dma is not None else self.DEFC[eng]
        op = _Op(eng, fn, tuple(r), tuple(w), dma, c, tbl, nb)
        op.grp = pm
        self.ops.append(op)

    def newgrp(self):
        self.grp += 1

    def _deps(self):
        ops = self.ops
        last_w, readers = {}, {}
        for i, op in enumerate(ops):
            deps = set()
            for r in op.r:
                if r in last_w:
                    deps.add(last_w[r])
            for w in op.w:
                if w in last_w:
                    deps.add(last_w[w])
                deps.update(readers.get(w, ()))
            deps.discard(i)
            op.deps = deps
            for r in op.r:
                readers.setdefault(r, []).append(i)
            for w in op.w:
                last_w[w] = i
                readers[w] = []

    def _list_schedule(self):
        ops = self.ops
        n = len(ops)
        succ = [[] for _ in range(n)]
        indeg = [0] * n
        for i, op in enumerate(ops):
            indeg[i] = len(op.deps)
            for j in op.deps:
                succ[j].append(i)
        est = [0.0] * n
        ready = {e: [] for e in self.ENG}
        for i, op in enumerate(ops):
            if indeg[i] == 0:
                ready[op.eng].append(i)
        free = {e: 0.0 for e in self.ENG}
        cur_tbl = 0
        cur_grp = -1
        dma_free = 0.0
        order = []
        done = 0
        while done < n:
            best = None
            for e in self.ENG:
                if not ready[e]:
                    continue
                fe = free[e]
                bi, bk = None, None
                for i in ready[e]:
                    stt = est[i] if est[i] > fe else fe
                    if e == "act" and ops[i].tbl and cur_tbl and ops[i].tbl != cur_tbl:
                        stt += self.TBL_SWITCH
                    if e == "pe" and ops[i].grp != cur_grp:
                        stt += self.GRP_SWITCH
                    k = (stt, i)
                    if bk is None or k < bk:
                        bk, bi = k, i
                if best is None or bk < best[0]:
                    best = (bk, e, bi)
            (stt, _), e, i = best
            op = ops[i]
            stt = max(est[i], free[e])
            if e == "act" and op.tbl and cur_tbl and op.tbl != cur_tbl:
                stt += self.TBL_SWITCH
            if e == "pe" and op.grp != cur_grp:
                stt += self.GRP_SWITCH
            ready[e].remove(i)
            if op.dma is not None:
                busy = self.ISSUE.get(e, 0.1)
                if op.nb:
                    beg = max(stt + busy + 1.5, dma_free)
                    dma_free = beg + op.nb / self.DMA_BW
                    op.t0, op.t1 = stt, dma_free + 0.5
                else:
                    op.t0, op.t1 = stt, stt + busy + op.c
                free[e] = stt + busy
            else:
                op.t0, op.t1 = stt, stt + op.c
                free[e] = op.t1
                if e == "act" and op.tbl:
                    cur_tbl = op.tbl
                if e == "pe":
                    cur_grp = op.grp
            order.append(i)
            done += 1
            for s_ in succ[i]:
                t_ = op.t1 + (self.XLAT if ops[s_].eng != e else 0.05)
                if t_ > est[s_]:
                    est[s_] = t_
                indeg[s_] -= 1
                if indeg[s_] == 0:
                    ready[ops[s_].eng].append(s_)
        self.sim_end = max(op.t1 for op in ops)
        return order

    def _resolve(self, reorder=True):
        self._deps()
        ops = self.ops
        if reorder:
            order = self._list_schedule()
            remap = {old: new for new, old in enumerate(order)}
            new_ops = [ops[i] for i in order]
            for op in new_ops:
                op.deps = {remap[j] for j in op.deps}
            self.ops = ops = new_ops
        dom, cnt = [], {}
        for op in ops:
            d = ("dma", op.dma) if op.dma is not None else op.eng
            cnt[d] = cnt.get(d, 0) + 1
            op.seq = cnt[d]
            dom.append(d)
        need = []
        for i, op in enumerate(ops):
            per = {}
            for j in op.deps:
                assert j < i
                d = dom[j]
                if d == "pe" and op.eng == "pe" and op.dma is None:
                    continue
                if d not in per or ops[j].seq > ops[per[d]].seq:
                    per[d] = j
            need.append(per)
            for d, j in per.items():
                ops[j].inc = True
        ctr = {}
        for i, op in enumerate(ops):
            d = dom[i]
            if op.dma is not None:
                ctr[d] = ctr.get(d, 0) + 16
                op.val = ctr[d]
                op.inc = True
            elif op.inc:
                ctr[d] = ctr.get(d, 0) + 1
                op.val = ctr[d]
        waited = {e: {} for e in self.ENG}
        for i, op in enumerate(ops):
            wl = []
            for d, j in need[i].items():
                v = ops[j].val
                if waited[op.eng].get(d, 0) >= v:
                    continue
                waited[op.eng][d] = v
                wl.append((d, v))
            op.waits = wl
        self.dom = dom
        return set(dom)

    def emit(self, final_wait_eng="sp"):
        nc = self.nc
        doms = self._resolve()
        last = {}
        for i, op in enumerate(self.ops):
            if op.dma is not None:
                last[self.dom[i]] = op.val
        with ExitStack() as es:
            sems = {}
            for d in sorted(doms, key=str):
                nm = "s_" + (d if isinstance(d, str) else "d_" + d[1])
                sems[d] = es.enter_context(nc.semaphore(nm))
            block = es.enter_context(nc.Block())
            by_eng = {e: [] for e in self.ENG}
            for i, op in enumerate(self.ops):
                by_eng[op.eng].append(i)

            def run(e, handle, idxs):
                for i in idxs:
                    op = self.ops[i]
                    for d, v in op.waits:
                        handle.wait_ge(sems[d], v)
                    ins = op.fn(handle)
                    if op.inc:
                        ins.then_inc(sems[self.dom[i]], 16 if op.dma is not None else 1)
                if e == final_wait_eng:
                    for d, v in last.items():
                        handle.wait_ge(sems[d], v)

            @block.tensor
            def _(h):
                run("pe", h, by_eng["pe"])

            @block.scalar
            def _(h):
                run("act", h, by_eng["act"])

            @block.vector
            def _(h):
                run("dve", h, by_eng["dve"])

            @block.gpsimd
            def _(h):
                run("pool", h, by_eng["pool"])

            @block.sync
            def _(h):
                run("sp", h, by_eng["sp"])


_CACHE = {}


def _merge(a, b):
    items = [((i + 0.5) / len(a), 0, i, f) for i, f in enumerate(a)]
    items += [((i + 0.5) / len(b), 1, i, f) for i, f in enumerate(b)]
    items.sort(key=lambda t: (t[0], t[1], t[2]))
    return [t[3] for t in items]


def build_program():
    nc = bass.Bass("TRN2", target_bir_lowering=False)

    def din(name, shape):
        return nc.dram_tensor(name, list(shape), F32, kind="ExternalInput").ap()

    def dout(name, shape):
        return nc.dram_tensor(name, list(shape), F32, kind="ExternalOutput").ap()

    xh = din("xh", [TOK + 128, DM])
    xs = din("xs", [NSMP, DM])
    ck = din("ck", [NSMP, 128, 128])
    cv = din("cv", [NSMP, 128, 128])
    sp_in = din("sp", [NSMP, 15, 512])
    w_in = din("w_in", [DM, NCOL])
    w_out = din("w_out", [DM, DM])
    w_pool = din("w_pool", [4, 128, 128])
    cst_d = din("cst", [128, NCST])
    nwb_d = din("nwb", [128, DM])

    y_d = dout("y", [TOK, DM])
    ys_d = dout("ys", [NSMP, DM])
    kl_d = dout("klast", [128, 128])
    vl_d = dout("vlast", [128, 128])
    ul_d = dout("ulast", [16, 512])
    nks_d = dout("nks", [NSMP, 128, 128])
    nvs_d = dout("nvs", [NSMP, 128, 128])
    nps_d = dout("nps", [NSMP, 15, 512])

    es = ExitStack()
    with es:
        def sb(name, shape, dt=F32):
            return es.enter_context(nc.sbuf_tensor(name, list(shape), dt))

        win = sb("win", [128, 8, NCOL], BF16)
        wout = sb("wout", [128, 8, DM], BF16)
        wpool = sb("wpool", [128, 4, 128], BF16)
        sga = sb("sga", [128, 4, TT])
        sgp = sb("sgp", [128, 4, TT])
        cst = sb("cstf", [128, NCST])
        cstb = sb("cstb", [128, NCB], BF16)
        nwb = sb("nwb_s", [128, DM])
        onesb = sb("onesb", [128, 128], BF16)
        esink = sb("esink", [128, 4])
        xb = [sb(f"xb{i}", [128, DM]) for i in range(3)]
        xr = [sb(f"xr{i}", [128, DM]) for i in range(2)]
        hn = [sb(f"hn{i}", [128, DM], BF16) for i in range(2)]
        st = [sb(f"st{i}", [128, 4]) for i in range(3)]
        hT = [sb(f"hT{i}", [128, 8, TT], BF16) for i in range(2)] + [sb("hT2", [128, 8, NSMP], BF16)]
        sq = sb("sq", [128, TT], BF16)
        lnb = sb("lnb", [128, TT])
        qTn = [sb(f"qTn{i}", [128, 4, TT], BF16) for i in range(2)] + [sb("qTn2", [128, 4, NSMP], BF16)]
        kTn = sb("kTn", [128, 128 + TOK], BF16)
        kTf = sb("kTf", [128, 128])
        Vall = sb("Vall", [128, TOK // 128 + 1, 128], BF16)
        vTf = sb("vTf", [128, TT])
        vlf = sb("vlf", [128, 128])
        uT = sb("uT", [128, 4, 16 + TT])
        pta = sb("pta", [128, 16 + TT])
        ptb = sb("ptb", [128, 16 + TT])
        dT = sb("dT", [128, 4, TT], BF16)
        dfix = sb("dfix", [128, 16])
        PT = [sb(f"PT{i}", [128, 2, 2, 128], BF16) for i in range(4)]
        rden = sb("rden", [128, TT])
        oT = sb("oT", [128, TT])
        mixT = sb("mixT", [128, 8, TT], BF16)
        ysb = [sb(f"ysb{i}", [128, DM]) for i in range(2)]
        ckb = sb("ckb", [128, NSMP, 128], BF16)
        cvb = sb("cvb", [128, NSMP, 128], BF16)
        KT = sb("KT", [128, NSMP, 128], BF16)
        spf = sb("spf", [120, 2, 512])
        uh = sb("uh", [128, 4, 2, 120])
        hs = sb("hs", [128, 4, NSMP])
        PTs = sb("PTs", [128, 2, NSMP, 4], BF16)
        pnf = sb("pnf", [16, 2, 4, NSMP])
        pnb = sb("pnb", [16, 2, 4, NSMP], BF16)
        vsf = sb("vsf", [16, 128])
        vsb = sb("vsb", [16, 128], BF16)
        ksf = sb("ksf", [16, 128])
        usf = sb("usf", [16, 512])
        rdens = sb("rdens", [128, NSMP, 4])
        oTs = sb("oTs", [128, NSMP, 4])
        fz = sb("fz", [128, 8])
        kTs = sb("kTs", [128, NSMP], BF16)
        kTfs = sb("kTfs", [128, NSMP])
        vTfs = sb("vTfs", [128, NSMP])
        sqs = sb("sqs", [128, NSMP], BF16)
        lnbs = sb("lnbs", [128, NSMP])
        sgas = sb("sgas", [128, 4, NSMP])
        sgps = sb("sgps", [128, 4, NSMP])
        uTs = sb("uTs", [128, 4, NSMP])
        dTs = sb("dTs", [128, 4, NSMP], BF16)
        mixTs = sb("mixTs", [128, 8, NSMP], BF16)

        B = [es.enter_context(nc.psum_tensor(f"B{i}", [128, 512], F32)) for i in range(8)]
        B3h = B[3][:].bitcast(BF16)
        B4h = B[4][:].bitcast(BF16)

        identb = cstb[:, C_ID:C_ID + 128]
        identf = cst[:, C_ID:C_ID + 128]
        blkb = cstb[:, C_BLK:C_BLK + 128]
        mask_pc = cstb[:, C_MP:C_MP + 256].rearrange("p (m q) -> p m q", m=2)
        mask_f = cstb[:, C_MF:C_MF + 128]
        mask_c = cstb[:, C_MC:C_MC + 128]
        epsc = cst[:, C_EPS:C_EPS + 1]

        S = Sched(nc)
        CP = lambda n: n / 2350.0 + 0.005
        CA = lambda n: (n + 130) / 1200.0
        CD = lambda n: n / 960.0 + 0.12

        S.add("sp", lambda e: e.dma_start(out=cst[:], in_=cst_d), w=["cst"], dma="cst")
        S.add("sp", lambda e: e.dma_start(out=nwb[:], in_=nwb_d), w=["nwb"], dma="nwb")
        S.add("dve", lambda e: e.tensor_copy(out=cstb[:], in_=cst[:, 0:NCB]), r=["cst"], w=["cstb"])
        S.add("pool", lambda e: e.memset(onesb[:], 1.0), w=["onesb"])
        S.add("act", lambda e: e.activation(out=esink[:], in_=cst[:, C_SK:C_SK + 4], func=AF.Exp),
              r=["cst"], w=["esink"])

        w_in_v = w_in.rearrange("(k p) c -> p k c", p=128)
        stg = [sga[:].rearrange("p c t -> p (c t)"), sgp[:].rearrange("p c t -> p (c t)")]
        stg_res = ["sga", "sgp"]
        WGROUPS = [(0, 2), (2, 4), (4, 6), (6, 8), (8, 10), (10, 12), (12, 14), (14, 16), (16, 18)]
        wg_of = {}
        nstg = [0]

        def stage_load(dst_ap3, src_ap3, nk, ncols, res_w, cast_eng, stg=stg, stg_res=stg_res, tag="stg"):
            i = nstg[0] % len(stg)
            nstg[0] += 1
            sv = stg[i][:, 0:nk * ncols].rearrange("p (k c) -> p k c", k=nk)
            S.add("sp", lambda e: e.dma_start(out=sv, in_=src_ap3), w=[stg_res[i]], dma=stg_res[i], nb=128 * nk * ncols * 4)
            if cast_eng == "act":
                S.add("act", lambda e: e.activation(out=dst_ap3, in_=sv, func=AF.Copy), r=[stg_res[i]], w=[res_w],
                      c=CA(nk * ncols))
            else:
                S.add("dve", lambda e: e.tensor_copy(out=dst_ap3, in_=sv), r=[stg_res[i]], w=[res_w],
                      c=nk * ncols / 1900.0 + 0.1)

        def load_w_in():
            for gi, (j0, j1) in enumerate(WGROUPS):
                for j in range(j0, j1):
                    wg_of[j] = f"wing{gi}"
                for kh in range(2):
                    S.add("pool", lambda e, j0=j0, j1=j1, kh=kh: e.dma_start(
                        out=win[:, 4 * kh:4 * kh + 4, j0 * 128:j1 * 128],
                        in_=w_in_v[:, 4 * kh:4 * kh + 4, j0 * 128:j1 * 128]),
                        w=[f"wing{gi}_{kh}"], dma=f"wing{gi}_{kh}", c=3.0 + gi * 5.0 + kh * 2.5)
                    for _ in range(2):
                        S.add("pe", lambda e: e.matmul(B[7][:, 0:512], onesb[:, 0:128], cstb[:, 0:512], start=True, stop=True),
                              r=["onesb", "cstb", f"wing{gi}_{kh}"], w=["B7"], c=0.25)

        def load_late_weights():
            w_out_v = w_out.rearrange("(m p) c -> p m c", p=128)
            stgB = [ysb[0][:], ysb[1][:], xr[0][:]]
            resB = ["ysb0", "ysb1", "xr0"]
            gate = [f"wing{len(WGROUPS) - 1}_1"]
            for m in range(8):
                S.add("pool", lambda e, m=m: e.dma_start(out=wout[:, m:m + 1, :], in_=w_out_v[:, m:m + 1, :]),
                      r=gate, w=[f"wout{m}"], dma=f"wout{m}", c=50.0 + 3.0 * m)
            S.add("pool", lambda e: e.dma_start(out=wpool[:], in_=w_pool.rearrange("g c d -> c g d")),
                  r=gate, w=["wpool"], dma="wpool", c=45.0)
            for q4 in range(4):
                bs = slice(4 * q4, 4 * q4 + 4)
                S.add("pool", lambda e, bs=bs: e.dma_start(out=ckb[:, bs, :], in_=ck[bs].rearrange("b j d -> j b d")),
                      r=["wout7"], w=["ckb"], dma="ckb", nb=1 << 20)
                S.add("pool", lambda e, bs=bs: e.dma_start(out=cvb[:, bs, :], in_=cv[bs].rearrange("b j d -> j b d")),
                      r=["wout7"], w=["cvb"], dma="cvb", nb=1 << 20)

        cnt = {"xb": 0, "hn": 0, "xr": 0, "ysb": 0, "zb": 0}

        def fence(tail_ap, rows, res, col):
            S.add("dve", lambda e: e.tensor_copy(out=fz[0:rows, col:col + 1], in_=tail_ap), r=[res], w=[f"fz{col}"], c=0.08)
            return f"fz{col}"

        def x_stage(src_ap, rows, hT_slot, col0):
            S.newgrp()
            xs_ = cnt["xb"] % 3
            cnt["xb"] += 1
            hs_ = cnt["hn"] % 2
            cnt["hn"] += 1
            X, H, ST = xb[xs_], hn[hs_], st[xs_]
            rx, rh, rst = f"xb{xs_}", f"hn{hs_}", f"st{xs_}"
            S.add("sp", lambda e: e.dma_start(out=X[0:rows, :], in_=src_ap), w=[rx], dma=rx, nb=rows * DM * 4)
            S.add("act", lambda e: e.activation(out=H[0:rows, :], in_=X[0:rows, :], func=AF.Square,
                                                accum_out=ST[0:rows, 0:1]), r=[rx], w=[rh, rst], c=CA(DM))
            S.add("act", lambda e: e.activation(out=ST[0:rows, 1:2], in_=ST[0:rows, 0:1], func=AF.Ln,
                                                scale=1.0 / DM, bias=epsc[0:rows]), r=[rst, "cst"], w=[rst], c=0.3, tbl=1)
            S.add("act", lambda e: e.activation(out=ST[0:rows, 2:3], in_=ST[0:rows, 1:2], func=AF.Exp,
                                                scale=-0.5), r=[rst], w=[rst], c=0.3, tbl=1)
            S.add("dve", lambda e: e.scalar_tensor_tensor(out=H[0:rows, :], in0=X[0:rows, :], scalar=ST[0:rows, 2:3],
                                                          in1=nwb[0:rows, :], op0=ALU.mult, op1=ALU.mult),
                  r=[rx, rst, "nwb"], w=[rh], c=CD(DM))
            for k in range(8):
                S.add("pe", lambda e, k=k: e.transpose(B3h[:, k * 128:k * 128 + rows],
                                                       H[0:rows, k * 128:(k + 1) * 128], identb[0:rows, 0:rows]),
                      r=[rh, "cstb"], w=["B3"], c=0.1)
            S.add("dve", lambda e: e.tensor_copy(
                out=hT[hT_slot][:, :, col0:col0 + rows],
                in_=B3h[:].rearrange("p (k t) -> p k t", k=8)[:, :, 0:rows]),
                r=["B3"], w=[f"hT{hT_slot}"], c=0.7)

        def inproj(j, hT_slot, T):
            S.newgrp()
            b_ = cnt["zb"] % 2
            cnt["zb"] += 1
            for k in range(8):
                S.add("pe", lambda e, k=k, b_=b_: e.matmul(B[b_][:, 0:T], win[:, k, j * 128:(j + 1) * 128],
                                                          hT[hT_slot][:, k, 0:T], start=(k == 0), stop=(k == 7)),
                      r=[f"{wg_of[j]}_{k // 4}", f"hT{hT_slot}"], w=[f"B{b_}"], c=CP(T))
            return B[b_], f"B{b_}"

        def qk_norm(bank, rb, T, wcol, out_ap, out_res, out_f32=None):
            sq_, rsq, lnb_, rln = (sq, "sq", lnb, "lnb") if T > NSMP else (sqs, "sqs", lnbs, "lnbs")
            _qk_norm(bank, rb, T, wcol, out_ap, out_res, out_f32, sq_, rsq, lnb_, rln)

        def _qk_norm(bank, rb, T, wcol, out_ap, out_res, out_f32, sq, rsq, lnb, rln):
            S.newgrp()
            S.add("act", lambda e: e.activation(out=sq[:, 0:T], in_=bank[:, 0:T], func=AF.Square), r=[rb], w=[rsq], c=CA(T))
            S.add("pe", lambda e: e.matmul(B[2][:, 0:T], blkb, sq[:, 0:T], start=True, stop=True),
                  r=[rsq, "cstb"], w=["B2"], c=CP(T))
            S.add("act", lambda e: e.activation(out=lnb[:, 0:T], in_=B[2][:, 0:T], func=AF.Ln, scale=1.0 / 64,
                                                bias=epsc), r=["B2", "cst"], w=[rln], c=CA(T), tbl=1)
            S.add("act", lambda e: e.activation(out=lnb[:, 0:T], in_=lnb[:, 0:T], func=AF.Exp, scale=-0.5),
                  r=[rln], w=[rln], c=CA(T), tbl=1)
            S.add("dve", lambda e: e.scalar_tensor_tensor(out=out_ap, in0=bank[:, 0:T], scalar=cst[:, wcol:wcol + 1],
                                                          in1=lnb[:, 0:T], op0=ALU.mult, op1=ALU.mult),
                  r=[rb, rln, "cst"], w=[out_res], c=CD(T))
            if out_f32 is not None:
                o_ap, o_res, c0, c1 = out_f32
                S.add("dve", lambda e: e.scalar_tensor_tensor(out=o_ap, in0=bank[:, c0:c1], scalar=cst[:, wcol:wcol + 1],
                                                              in1=lnb[:, c0:c1], op0=ALU.mult, op1=ALU.mult),
                      r=[rb, rln, "cst"], w=[o_res], c=0.25)

        def proj_k(hT_slot, T, kcol0, last_f32):
            rows = min(T, 128)
            bank, rb = inproj(0, hT_slot, T)
            if T == NSMP:
                qk_norm(bank, rb, T, C_KW, kTs[:, 0:T], "kTs", out_f32=(kTfs[:, 0:T], "kTfs", 0, T))
                return
            qk_norm(bank, rb, T, C_KW, kTn[:, kcol0:kcol0 + T], "kTn",
                    out_f32=(kTf[:, 0:rows], "kTf", T - rows, T) if last_f32 else None)

        def proj_q(c, hT_slot, T, qs):
            bank, rb = inproj(6 + c, hT_slot, T)
            qk_norm(bank, rb, T, C_QW, qTn[qs][:, c, 0:T], f"qTn{qs}")

        def proj_v(hT_slot, T, vblk0, last_f32):
            nblk = max(T // 128, 1)
            rows = min(T, 128)
            bank, rb = inproj(1, hT_slot, T)
            vT_, rvT = (vTf, "vTf") if T > NSMP else (vTfs, "vTfs")
            S.newgrp()
            S.add("act", lambda e: e.activation(out=vT_[:, 0:T], in_=bank[:, 0:T], func=AF.Copy), r=[rb], w=[rvT], c=CA(T))
            for bi in range(nblk):
                S.add("pe", lambda e, bi=bi: e.transpose(B[2][0:rows, bi * 128:(bi + 1) * 128],
                                                         vT_[:, bi * 128:bi * 128 + rows], identf),
                      r=[rvT, "cst"], w=["B2"], c=0.27)
            if T >= 128:
                S.add("dve", lambda e: e.tensor_copy(
                    out=Vall[:, vblk0:vblk0 + nblk, :],
                    in_=B[2][:, 0:nblk * 128].rearrange("p (b d) -> p b d", b=nblk)), r=["B2"], w=["Vall"], c=CD(T))
                if last_f32:
                    S.add("dve", lambda e: e.tensor_copy(out=vlf[:], in_=B[2][:, (nblk - 1) * 128:nblk * 128]),
                          r=["B2"], w=["vlf"], c=0.25)
            else:
                S.add("dve", lambda e: e.tensor_copy(out=vsf[:], in_=B[2][0:rows, 0:128]), r=["B2"], w=["vsf"], c=0.25)
                S.add("dve", lambda e: e.tensor_copy(out=vsb[:], in_=B[2][0:rows, 0:128]), r=["B2"], w=["vsb"], c=0.25)

        def proj_u(g, hT_slot, T, ucol0, halo=False):
            bank, rb = inproj(2 + g, hT_slot, T)
            if T == NSMP:
                S.add("act", lambda e: e.activation(out=uTs[:, g, 0:T], in_=bank[:, 0:T], func=AF.Copy),
                      r=[rb], w=[f"uTs{g}"], c=CA(T))
            elif not halo:
                S.add("act", lambda e: e.activation(out=uT[:, g, ucol0:ucol0 + T], in_=bank[:, 0:T], func=AF.Copy),
                      r=[rb], w=[f"uT{g}"], c=CA(T))
            else:
                S.add("act", lambda e: e.activation(out=uT[:, g, 0:16], in_=bank[:, T - 16:T], func=AF.Copy),
                      r=[rb], w=["uTh"], c=0.3)

        def proj_ga(c, hT_slot, T):
            bank, rb = inproj(10 + c, hT_slot, T)
            dst, rd = (sga, "sga") if T > NSMP else (sgas, "sgas")
            S.add("act", lambda e: e.activation(out=dst[:, c, 0:T], in_=bank[:, 0:T], func=AF.Silu), r=[rb], w=[rd], c=CA(T), tbl=2)

        def proj_gp(g, hT_slot, T):
            bank, rb = inproj(14 + g, hT_slot, T)
            dst, rd = (sgp, "sgp") if T > NSMP else (sgps, "sgps")
            S.add("act", lambda e: e.activation(out=dst[:, g, 0:T], in_=bank[:, 0:T], func=AF.Silu), r=[rb], w=[rd], c=CA(T), tbl=2)

        def attn_scores(bi, cp, kcol, first, qs):
            S.newgrp()
            q0 = bi * 128
            for h in range(2):
                hp = slice(64 * h, 64 * h + 64)
                sb_ = B[4 + h]
                for pc in range(2):
                    kc = kcol - 128 + 128 * pc
                    S.add("pe", lambda e, hp=hp, sb_=sb_, pc=pc, kc=kc, h=h: e.matmul(
                        sb_[:, pc * 256:(pc + 1) * 256].rearrange("p (c q) -> p c q", c=2),
                        kTn[hp, kc:kc + 128], qTn[qs][hp, 2 * cp:2 * cp + 2, q0:q0 + 128],
                        start=True, stop=True, tile_position=(64 * h, 0)),
                        r=["kTn", f"qTn{qs}"], w=[f"B{4 + h}"], c=0.15, pm=1)
                pt = PT[2 * cp + h]
                rpt = f"PT{2 * cp + h}"
                S.add("act", lambda e, sb_=sb_, pt=pt: e.activation(
                    out=pt[:].rearrange("p a c q -> p (a c q)"), in_=sb_[:], func=AF.Exp, scale=0.125),
                    r=[f"B{4 + h}"], w=[rpt], c=0.55, tbl=1)
                if first:
                    S.add("dve", lambda e, pt=pt: e.tensor_tensor(
                        out=pt[:, 0], in0=pt[:, 0], in1=mask_f.unsqueeze(1).to_broadcast([128, 2, 128]),
                        op=ALU.mult), r=[rpt, "cstb"], w=[rpt], c=0.3)
                    S.add("dve", lambda e, pt=pt: e.tensor_tensor(
                        out=pt[:, 1], in0=pt[:, 1], in1=mask_c.unsqueeze(1).to_broadcast([128, 2, 128]),
                        op=ALU.mult), r=[rpt, "cstb"], w=[rpt], c=0.3)
                else:
                    S.add("dve", lambda e, pt=pt: e.tensor_tensor(
                        out=pt[:], in0=pt[:], in1=mask_pc.unsqueeze(2).to_broadcast([128, 2, 2, 128]),
                        op=ALU.mult), r=[rpt, "cstb"], w=[rpt], c=0.45)

        def attn_pv(bi, cp, vblk):
            S.newgrp()
            for h in range(2):
                hp = slice(64 * h, 64 * h + 64)
                pt = PT[2 * cp + h]
                rpt = f"PT{2 * cp + h}"
                for pc in range(2):
                    S.add("pe", lambda e, hp=hp, pt=pt, pc=pc, h=h: e.matmul(
                        B[7][hp, cp * 256:(cp + 1) * 256].rearrange("p (c q) -> p c q", c=2),
                        onesb[:, 0:64], pt[:, pc], start=(pc == 0), stop=(pc == 1),
                        tile_position=(0, 64 * h)), r=[rpt, "onesb"], w=["B7"], c=0.12, pm=2)
                for ci in range(2):
                    c = 2 * cp + ci
                    for pc in range(2):
                        vb = vblk - 1 + pc
                        S.add("pe", lambda e, hp=hp, pt=pt, pc=pc, ci=ci, c=c, vb=vb, h=h: e.matmul(
                            B[6][hp, c * 128:(c + 1) * 128], Vall[:, vb, 64 * h:64 * h + 64], pt[:, pc, ci, :],
                            start=(pc == 0), stop=(pc == 1), tile_position=(0, 64 * h)),
                            r=[rpt, "Vall"], w=["B6"], c=0.06, pm=2)

        def attn_epi(bi):
            S.newgrp()
            q0 = bi * 128
            S.add("dve", lambda e: e.tensor_tensor(
                out=rden[:].rearrange("p (c q) -> p c q", c=4), in0=B[7][:].rearrange("p (c q) -> p c q", c=4),
                in1=esink[:].unsqueeze(2).to_broadcast([128, 4, 128]), op=ALU.add), r=["B7", "esink"], w=["rden"], c=0.65)
            S.add("act", lambda e: e.activation(out=rden[:], in_=rden[:], func=AF.Ln), r=["rden"], w=["rden"], c=0.6, tbl=1)
            S.add("act", lambda e: e.activation(out=rden[:], in_=rden[:], func=AF.Exp, scale=-1.0), r=["rden"], w=["rden"], c=0.6, tbl=1)
            S.add("dve", lambda e: e.tensor_tensor(out=oT[:], in0=B[6][:], in1=rden[:], op=ALU.mult),
                  r=["B6", "rden"], w=["oT"], c=0.65)
            S.add("pool", lambda e: e.tensor_tensor(
                out=mixT[:, 0:4, q0:q0 + 128], in0=oT[:].rearrange("p (c q) -> p c q", c=4),
                in1=sga[:, :, q0:q0 + 128], op=ALU.mult), r=["oT", "sga"], w=[f"mixTa{bi}"], c=1.3)

        def pool_group(g, T, first_tile, bank_i):
            w_ = POOLW[g]
            U = uT[:, g, :]
            src, src_res = U, None
            sh, bi, lo = 1, 0, 0
            bufs = [pta, ptb]
            while sh < w_:
                dst = bufs[bi]
                dres = "pta" if bi == 0 else "ptb"
                lo2 = lo + sh
                rr = [f"uT{g}", "uTh"] if src_res is None else [src_res]
                S.add("pool", lambda e, dst=dst, src=src, lo2=lo2, sh=sh: e.tensor_tensor(
                    out=dst[:, lo2:16 + T], in0=src[:, lo2:16 + T], in1=src[:, lo2 - sh:16 + T - sh], op=ALU.add),
                    r=rr, w=[dres], c=1.1)
                src, src_res = dst, dres
                lo = lo2
                sh *= 2
                bi ^= 1
            S.add("dve", lambda e, src=src: e.scalar_tensor_tensor(
                out=dT[:, g, 0:T], in0=src[:, 16:16 + T], scalar=1.0 / w_, in1=uT[:, g, 16:16 + T],
                op0=ALU.mult, op1=ALU.subtract), r=[src_res, f"uT{g}"], w=["dT"], c=CD(T))
            if first_tile:
                S.add("dve", lambda e, src=src: e.tensor_tensor(
                    out=dfix[:], in0=src[:, 16:32], in1=cst[:, C_RC + 16 * g:C_RC + 16 * g + 16], op=ALU.mult),
                    r=[src_res, "cst"], w=["dfix"], c=0.15)
                S.add("dve", lambda e: e.tensor_tensor(
                    out=dT[:, g, 0:16], in0=dfix[:], in1=uT[:, g, 16:32], op=ALU.subtract),
                    r=["dfix", f"uT{g}"], w=["dT"], c=0.15)
            pool_proj(g, T, bank_i)

        def pool_proj(g, T, bank_i):
            S.newgrp()
            dT_, rdT, mx, rmx, sg, rsg = ((dT, "dT", mixT, "mixTp", sgp, "sgp") if T > NSMP
                                          else (dTs, "dTs", mixTs, "mixTs", sgps, "sgps"))
            S.add("pe", lambda e: e.matmul(B[bank_i][:, 0:T], wpool[:, g, :], dT_[:, g, 0:T], start=True, stop=True),
                  r=["wpool", rdT], w=[f"B{bank_i}"], c=CP(T))
            S.add("dve", lambda e: e.scalar_tensor_tensor(
                out=mx[:, 4 + g, 0:T], in0=B[bank_i][:, 0:T], scalar=cst[:, C_PS + g:C_PS + g + 1],
                in1=sg[:, g, 0:T], op0=ALU.mult, op1=ALU.mult), r=[f"B{bank_i}", "cst", rsg], w=[rmx], c=CD(T))

        def out_block(col0, rows, x_src, y_dst, banks):
            S.newgrp()
            smp = rows == NSMP
            mx = mixTs if smp else mixT
            for nh in range(2):
                bk = banks[nh]
                for m in range(8):
                    rmx = "mixTs" if smp else (f"mixTa{col0 // 128}" if m < 4 else "mixTp")
                    S.add("pe", lambda e, nh=nh, m=m, bk=bk: e.matmul(
                        B[bk][0:rows, :], mx[:, m, col0:col0 + rows], wout[:, m, nh * 512:(nh + 1) * 512],
                        start=(m == 0), stop=(m == 7)),
                        r=[rmx, f"wout{m}"], w=[f"B{bk}"], c=0.225)
            rs_ = cnt["xr"] % 2
            cnt["xr"] += 1
            ys_ = cnt["ysb"] % 2
            cnt["ysb"] += 1
            XR, Y = xr[rs_], ysb[ys_]
            S.add("sp", lambda e: e.dma_start(out=XR[0:rows, :], in_=x_src), w=[f"xr{rs_}"], dma=f"xr{rs_}", nb=rows * DM * 4)
            for nh in range(2):
                bk = banks[nh]
                S.add("dve", lambda e, nh=nh, bk=bk: e.tensor_tensor(
                    out=Y[0:rows, nh * 512:(nh + 1) * 512], in0=B[bk][0:rows, :],
                    in1=XR[0:rows, nh * 512:(nh + 1) * 512], op=ALU.add),
                    r=[f"B{bk}", f"xr{rs_}"], w=[f"ysb{ys_}"], c=0.65)
            fr = fence(Y[0:rows, DM - 1:DM], rows, f"ysb{ys_}", ys_)
            S.add("pool" if rows == 128 else "sp", lambda e: e.dma_start(out=y_dst, in_=Y[0:rows, :]),
                  r=[f"ysb{ys_}", fr], dma=f"ysb{ys_}", nb=rows * DM * 4)

        def B_steps(ti):
            hs_ = (ti + 1) % 2
            qs = ti % 2
            t0 = ti * TT
            last = ti == NTILE - 1
            X = []
            for bi in range(4):
                X.append(lambda bi=bi: x_stage(xh[128 + t0 + bi * 128:128 + t0 + (bi + 1) * 128, :], 128, hs_, bi * 128))
            G1 = []
            G1.append(lambda: proj_k(hs_, TT, 128 + t0, last))
            G1.append(lambda: proj_v(hs_, TT, 1 + ti * 4, last))
            for c in range(2):
                G1.append(lambda c=c: proj_q(c, hs_, TT, qs))
            if last:
                def _kv_out():
                    f1 = fence(kTf[:, 127:128], 128, "kTf", 2)
                    f2 = fence(vlf[:, 127:128], 128, "vlf", 3)
                    S.add("sp", lambda e: e.dma_start(out=kl_d, in_=kTf[:]), r=["kTf", f1], dma="kl")
                    S.add("sp", lambda e: e.dma_start(out=vl_d, in_=vlf[:]), r=["vlf", f2], dma="vl")
                G1.append(_kv_out)
            G2 = [lambda c=c: proj_q(c, hs_, TT, qs) for c in range(2, 4)]
            G2 += [lambda c=c: proj_ga(c, hs_, TT) for c in range(4)]
            G3 = []
            if ti > 0:
                G3.append(lambda: S.add("pool", lambda e: e.tensor_copy(out=uT[:, :, 0:16], in_=uT[:, :, TT:TT + 16]),
                                        r=[f"uT{g}" for g in range(4)], w=["uTh"]))
            for g in range(4):
                G3.append(lambda g=g: proj_u(g, hs_, TT, 16))
            for g in range(4):
                G3.append(lambda g=g: proj_gp(g, hs_, TT))
            return X, G1, G2, G3

        def A_steps(ti):
            qs = ti % 2
            t0 = ti * TT
            last = ti == NTILE - 1
            sc = []
            for s in range(8):
                bi, cp = s // 2, s % 2
                sc.append((bi, cp))
            P1 = []
            def SC(s):
                bi, cp = sc[s]
                return lambda: attn_scores(bi, cp, 128 + t0 + bi * 128, ti == 0 and bi == 0, qs)
            def DO(s):
                bi, cp = sc[s]
                return lambda: attn_pv(bi, cp, 1 + ti * 4 + bi)
            P1.append(SC(0))
            for s in range(8):
                if s + 1 < 8:
                    P1.append(SC(s + 1))
                P1.append(DO(s))
                if sc[s][1] == 1:
                    P1.append(lambda bi=sc[s][0]: attn_epi(bi))
                    if last:
                        bi = sc[s][0]
                        r0 = t0 + bi * 128
                        P1.append(lambda bi=bi, r0=r0: out_block(bi * 128, 128, xh[128 + r0:128 + r0 + 128, :],
                                                                 y_d[r0:r0 + 128, :], (0, 1) if bi % 2 == 0 else (2, 3)))
            P2 = [lambda g=g: pool_group(g, TT, ti == 0, (2 + (g % 2)) if last else (4 + (g % 2))) for g in range(4)]
            P3 = []
            for bi in range(4):
                r0 = t0 + bi * 128
                if not last:
                    P3.append(lambda bi=bi, r0=r0: out_block(bi * 128, 128, xh[128 + r0:128 + r0 + 128, :],
                                                             y_d[r0:r0 + 128, :], (6, 7) if bi % 2 == 0 else (4, 5)))
            if last:
                def _u_out():
                    for g in range(4):
                        S.add("pe", lambda e, g=g: e.transpose(B[2][0:16, g * 128:(g + 1) * 128],
                                                               uT[:, g, TT:TT + 16], identf),
                              r=[f"uT{g}", "cst"], w=["B2"])
                    S.add("dve", lambda e: e.tensor_copy(out=usf[:], in_=B[2][0:16, :]), r=["B2"], w=["usf"])
                    f3 = fence(usf[:, 511:512], 16, "usf", 4)
                    S.add("sp", lambda e: e.dma_start(out=ul_d, in_=usf[:]), r=["usf", f3], dma="ul")
                P2.append(_u_out)
            return P1, P2, P3

        def run(steps):
            for f in steps:
                S.newgrp()
                f()

        x_stage(xh[0:128, :], 128, 0, 0)
        load_w_in()
        proj_k(0, 128, 0, False)
        proj_v(0, 128, 0, False)
        for g in range(4):
            proj_u(g, 0, 128, 0, halo=True)
        BS = [B_steps(ti) for ti in range(NTILE)]
        X, G1, G2, G3 = BS[0]
        run(X)
        run(G1)
        load_late_weights()
        run(G2)
        run(_merge(G3, BS[1][0]))
        for ti in range(NTILE):
            P1, P2, P3 = A_steps(ti)
            if ti + 1 < NTILE:
                _, G1, G2, G3 = BS[ti + 1]
                if ti + 2 < NTILE:
                    G3 = G3 + BS[ti + 2][0]
                if ti + 2 == NTILE:
                    G3 = G3 + sample_steps(nc, S, locals())
                run(_merge(P1, G1))
                run(_merge(P2, G2))
                run(_merge(P3, G3))
            else:
                run(P2)
                run(P1)
                run(P3)

        S.emit()
        _CACHE["sched"] = S
    return nc


def sample_steps(nc, S, L):
    (xs, ck, cv, sp_in, ys_d, nks_d, nvs_d, nps_d) = (L[k] for k in
                                                    ("xs", "ck", "cv", "sp_in", "ys_d", "nks_d", "nvs_d", "nps_d"))
    B, cst, cstb, onesb, esink = L["B"], L["cst"], L["cstb"], L["onesb"], L["esink"]
    identf, identb, B3h = L["identf"], L["identb"], L["B3h"]
    ckb, cvb, KT, spf, uh, hs = L["ckb"], L["cvb"], L["KT"], L["spf"], L["uh"], L["hs"]
    PTs, pnf, pnb, vsf, vsb, ksf, usf = L["PTs"], L["pnf"], L["pnb"], L["vsf"], L["vsb"], L["ksf"], L["usf"]
    rdens, oTs = L["rdens"], L["oTs"]
    kTs, kTfs, uTs, dTs, mixTs, sgas = L["kTs"], L["kTfs"], L["uTs"], L["dTs"], L["mixTs"], L["sgas"]
    qTn = L["qTn"][2]
    N = NSMP
    SM = {"pe": 0.07, "act": 0.35, "dve": 0.25, "pool": 0.5, "sp": 3.0}

    def add(eng, fn, r=(), w=(), dma=None, c=None):
        if c is None:
            c = SM["sp"] if dma is not None else SM[eng]
        S.add(eng, fn, r=r, w=w, dma=dma, c=c, tbl=1 if eng == "act" else 0)

    steps = []

    def s_copies():
        add("sp", lambda e: e.dma_start(out=nks_d[:, 0:127, :], in_=ck[:, 1:128, :]), r=["ckb"], dma="nks_c")
        add("sp", lambda e: e.dma_start(out=nvs_d[:, 0:127, :], in_=cv[:, 1:128, :]), r=["cvb"], dma="nvs_c")
        add("sp", lambda e: e.dma_start(out=nps_d[:, 0:14, :], in_=sp_in[:, 1:15, :]), r=["cvb"], dma="nps_c")
        add("sp", lambda e: e.dma_start(out=spf[:], in_=sp_in.rearrange("(a b) t c -> (b t) a c", a=2)),
            r=["cvb"], w=["spf"], dma="spf")
    steps.append(s_copies)

    def s_kt(q8):
        def f():
            for i in range(8):
                b = q8 * 8 + i
                add("pe", lambda e, b=b, i=i: e.transpose(B3h[:, i * 128:(i + 1) * 128], ckb[:, b, :], identb),
                    r=["ckb", "cstb"], w=["B3"], c=0.08)
            add("dve", lambda e: e.tensor_copy(out=KT[:, q8 * 8:(q8 + 1) * 8, :],
                                               in_=B3h[:].rearrange("p (b j) -> p b j", b=8)),
                r=["B3"], w=["KT"], c=0.7)
        return f
    steps.append(s_kt(0))
    steps.append(s_kt(1))

    def s_hist(a):
        def f():
            for g in range(4):
                add("pe", lambda e, g=g: e.transpose(B[a][:, g * 128:g * 128 + 120],
                                                     spf[:, a, g * 128:(g + 1) * 128], identf[0:120, 0:120]),
                    r=["spf", "cst"], w=[f"B{a}"], c=0.27)
            add("dve", lambda e: e.tensor_copy(
                out=uh[:, :, a, :], in_=B[a][:].rearrange("p (g t) -> p g t", g=4)[:, :, 0:120]),
                r=[f"B{a}"], w=["uh"], c=0.6)
        return f
    steps.append(s_hist(0))
    steps.append(s_hist(1))

    steps.append(lambda: L["x_stage"](xs, N, 2, 0))
    steps.append(lambda: L["proj_k"](2, N, 0, True))
    for c in range(4):
        steps.append(lambda c=c: L["proj_q"](c, 2, N, 2))
    steps.append(lambda: L["proj_v"](2, N, 0, False))
    for g in range(4):
        steps.append(lambda g=g: L["proj_u"](g, 2, N, 0))
    for c in range(4):
        steps.append(lambda c=c: L["proj_ga"](c, 2, N))
    for g in range(4):
        steps.append(lambda g=g: L["proj_gp"](g, 2, N))

    def s_newrows():
        add("pe", lambda e: e.transpose(B[2][0:N, 0:128], kTfs[:, 0:N], identf), r=["kTfs", "cst"], w=["B2"], c=0.27)
        add("dve", lambda e: e.tensor_copy(out=ksf[:], in_=B[2][0:N, 0:128]), r=["B2"], w=["ksf"])
        f5 = L["fence"](ksf[:, 127:128], N, "ksf", 5)
        f6 = L["fence"](vsf[:, 127:128], N, "vsf", 6)
        add("sp", lambda e: e.dma_start(out=nks_d[:, 127, :], in_=ksf[:]), r=["ksf", f5], dma="nks_n")
        add("sp", lambda e: e.dma_start(out=nvs_d[:, 127, :], in_=vsf[:]), r=["vsf", f6], dma="nvs_n")
        for g in range(4):
            add("pe", lambda e, g=g: e.transpose(B[2][0:N, g * 128:(g + 1) * 128], uTs[:, g, 0:N], identf),
                r=[f"uTs{g}", "cst"], w=["B2"], c=0.27)
        add("dve", lambda e: e.tensor_copy(out=usf[:], in_=B[2][0:N, :]), r=["B2"], w=["usf"])
        f7 = L["fence"](usf[:, 511:512], N, "usf", 7)
        add("sp", lambda e: e.dma_start(out=nps_d[:, 14, :], in_=usf[:]), r=["usf", f7], dma="nps_n")
    steps.append(s_newrows)

    def s_scores(h):
        def f():
            hp = slice(64 * h, 64 * h + 64)
            bank = B[h]
            rb = f"B{h}"
            for b in range(N):
                add("pe", lambda e, b=b: e.matmul(
                    bank[:, b * 4:(b + 1) * 4], KT[hp, b, :], qTn[hp, :, b], start=True, stop=True,
                    tile_position=(64 * h, 0)), r=["KT", "qTn2"], w=[rb])
            add("pe", lambda e: e.matmul(
                bank[0:N, 64:128].rearrange("p (c b) -> p c b", c=4), kTs[hp, 0:N], qTn[hp, :, 0:N],
                start=True, stop=True, tile_position=(64 * h, 0)), r=["kTs", "qTn2"], w=[rb])
            add("act", lambda e: e.activation(
                out=PTs[:, h].rearrange("p b c -> p (b c)"), in_=bank[:, 0:64], func=AF.Exp, scale=0.125),
                r=[rb], w=[f"PTs{h}"])
            add("act", lambda e: e.activation(
                out=pnf[:, h].rearrange("p c b -> p (c b)"), in_=bank[0:N, 64:128], func=AF.Exp, scale=0.125),
                r=[rb], w=[f"pnf{h}"])
            add("dve", lambda e: e.tensor_tensor(
                out=pnb[:, h].rearrange("p c b -> p (c b)"), in0=pnf[:, h].rearrange("p c b -> p (c b)"),
                in1=cst[0:N, C_DM:C_DM + 64], op=ALU.mult), r=[f"pnf{h}", "cst"], w=[f"pnb{h}"])
        return f
    steps.append(s_scores(0))
    steps.append(s_scores(1))

    def s_pv(h):
        def f():
            hp = slice(64 * h, 64 * h + 64)
            add("pe", lambda e: e.matmul(
                B[2][hp, 0:64].rearrange("p (b c) -> p b c", b=N), onesb[0:N, 0:64],
                pnb[:, h].rearrange("p c b -> p b c"), start=True, stop=False, tile_position=(0, 64 * h),
                skip_group_check=True), r=[f"pnb{h}", "onesb"], w=["B2"])
            add("pe", lambda e: e.matmul(
                B[2][hp, 0:64], onesb[:, 0:64], PTs[:, h].rearrange("p b c -> p (b c)"),
                start=False, stop=True, tile_position=(0, 64 * h), skip_group_check=True),
                r=[f"PTs{h}", "onesb"], w=["B2"])
            add("pe", lambda e: e.matmul(
                B[3][hp, 0:64].rearrange("p (b c) -> p b c", b=N), vsb[:, 64 * h:64 * h + 64],
                pnb[:, h].rearrange("p c b -> p b c"), start=True, stop=False, tile_position=(0, 64 * h),
                skip_group_check=True), r=[f"pnb{h}", "vsb"], w=["B3"])
            for b in range(N):
                add("pe", lambda e, b=b: e.matmul(
                    B[3][hp, b * 4:(b + 1) * 4], cvb[:, b, 64 * h:64 * h + 64], PTs[:, h, b, :],
                    start=False, stop=(b == N - 1), tile_position=(0, 64 * h), skip_group_check=True),
                    r=[f"PTs{h}", "cvb"], w=["B3"])
        return f

    def s_epi():
        add("dve", lambda e: e.tensor_tensor(
            out=rdens[:], in0=B[2][:, 0:64].rearrange("p (b c) -> p b c", b=N),
            in1=esink[:].unsqueeze(1).to_broadcast([128, N, 4]), op=ALU.add), r=["B2", "esink"], w=["rdens"])
        add("dve", lambda e: e.reciprocal(out=rdens[:], in_=rdens[:]), r=["rdens"], w=["rdens"], c=0.5)
        add("dve", lambda e: e.tensor_tensor(out=oTs[:], in0=B[3][:, 0:64].rearrange("p (b c) -> p b c", b=N),
                                             in1=rdens[:], op=ALU.mult), r=["B3", "rdens"], w=["oTs"])
        add("dve", lambda e: e.tensor_tensor(out=mixTs[:, 0:4, 0:N], in0=oTs[:].rearrange("p b c -> p c b"),
                                             in1=sgas[:, :, 0:N], op=ALU.mult), r=["oTs", "sgas"], w=["mixTs"])
    def s_pv_epi():
        s_pv(0)()
        s_pv(1)()
        s_epi()
    steps.append(s_pv_epi)

    def s_pool(g):
        def f():
            w_ = POOLW[g]
            add("dve", lambda e: e.tensor_reduce(
                out=hs[:, g, :], in_=uh[:, g].rearrange("p a (b t) -> p (a b) t", t=15)[:, :, 16 - w_:15],
                axis=AX.X, op=ALU.add), r=["uh"], w=[f"hs{g}"])
            add("dve", lambda e: e.tensor_tensor(out=hs[:, g, :], in0=hs[:, g, :], in1=uTs[:, g, 0:N],
                                                 op=ALU.add), r=[f"hs{g}", f"uTs{g}"], w=[f"hs{g}"])
            add("dve", lambda e: e.scalar_tensor_tensor(
                out=dTs[:, g, 0:N], in0=hs[:, g, :], scalar=1.0 / w_, in1=uTs[:, g, 0:N],
                op0=ALU.mult, op1=ALU.subtract), r=[f"hs{g}", f"uTs{g}"], w=["dTs"])
            L["pool_proj"](g, N, g % 2)
        return f
    for g in range(4):
        steps.append(s_pool(g))
    steps.append(lambda: L["out_block"](0, N, xs, ys_d, (0, 1)))
    return steps


def _col_perm():
    p = list(range(512, 768))
    p += list(range(1280, 1792))
    for c in range(4):
        p += list(range(c * 64, c * 64 + 64)) + list(range((c + 4) * 64, (c + 4) * 64 + 64))
    for c in range(4):
        p += list(range(768 + c * 64, 768 + c * 64 + 64)) + list(range(768 + (c + 4) * 64, 768 + (c + 4) * 64 + 64))
    p += list(range(1792, 2304))
    return np.array(p)


def _row_perm():
    p = []
    for c in range(4):
        p += list(range(c * 64, c * 64 + 64)) + list(range((c + 4) * 64, (c + 4) * 64 + 64))
    p += list(range(512, 1024))
    return np.array(p)


def _consts(core, norm_w, q_norm_w, k_norm_w, sinks, pool_scale):
    c = np.zeros((128, NCST), np.float32)
    j = np.arange(128)[:, None]
    i = np.arange(128)[None, :]
    c[:, C_ID:C_ID + 128] = np.eye(128, dtype=np.float32)
    c[:, C_MC:C_MC + 128] = (j <= i)
    c[:, C_MP:C_MP + 128] = (j >= i)
    c[:, C_MF:C_MF + 128] = 0.0 if core == 0 else (j >= i)
    c[:, C_BLK:C_BLK + 128] = ((j // 64) == (i // 64))
    c[:, C_QW] = np.tile(q_norm_w, 2)
    c[:, C_KW] = np.tile(k_norm_w, 2)
    c[:, C_NW:C_NW + 8] = norm_w.reshape(8, 128).T
    c[:, C_PS:C_PS + 4] = pool_scale.reshape(4, 128).T
    for cc in range(4):
        c[0:64, C_SK + cc] = sinks[cc]
        c[64:128, C_SK + cc] = sinks[cc + 4]
    for g, w in enumerate(POOLW):
        pos = np.arange(16)
        cntv = np.minimum(w, pos + 1) if core == 0 else np.full(16, w)
        c[:, C_RC + 16 * g:C_RC + 16 * g + 16] = (1.0 / cntv)[None, :]
    dm = np.zeros((16, 4, 16), np.float32)
    for b in range(16):
        dm[b, :, b] = 1.0
    c[0:16, C_DM:C_DM + 64] = dm.reshape(16, 64)
    c[:, C_EPS] = EPS
    c[:, C_M1] = -1.0
    return c


def kernel(x_prompt, x_sample, cache_k, cache_v, state_pool, norm_w, w_in, q_norm_w, k_norm_w, sinks,
           w_pool, pool_scale, w_out):
    f = lambda a: np.ascontiguousarray(np.asarray(a, dtype=np.float32))
    xp = f(x_prompt)[0]
    xsm = f(x_sample)[:, 0, :]
    ckh = f(cache_k)[0].reshape(128, 128, 128)
    cvh = f(cache_v)[0].reshape(128, 128, 128)
    sph = f(state_pool)[0]
    w_in_p = np.ascontiguousarray(f(w_in)[0][:, _col_perm()])
    w_out_p = np.ascontiguousarray(f(w_out)[0][_row_perm(), :])
    w_pool_h = f(w_pool)[0]
    if "nc" not in _CACHE:
        _CACHE["nc"] = build_program()
    nc = _CACHE["nc"]
    nwb_h = np.ascontiguousarray(np.tile(f(norm_w)[0][None, :], (128, 1)))
    in_maps = []
    for c in range(NCORE):
        xhc = np.zeros((TOK + 128, DM), np.float32)
        lo = c * TOK
        if c > 0:
            xhc[0:128] = xp[lo - 128:lo]
        xhc[128:] = xp[lo:lo + TOK]
        in_maps.append({
            "xh": xhc,
            "xs": np.ascontiguousarray(xsm[c * NSMP:(c + 1) * NSMP]),
            "ck": np.ascontiguousarray(ckh[c * NSMP:(c + 1) * NSMP]),
            "cv": np.ascontiguousarray(cvh[c * NSMP:(c + 1) * NSMP]),
            "sp": np.ascontiguousarray(sph[c * NSMP:(c + 1) * NSMP]),
            "w_in": w_in_p, "w_out": w_out_p, "w_pool": w_pool_h,
            "cst": _consts(c, f(norm_w)[0], f(q_norm_w)[0], f(k_norm_w)[0], f(sinks)[0], f(pool_scale)[0]),
            "nwb": nwb_h,
        })
    res = run_bass_kernel_spmd(nc, in_maps, core_ids=list(range(NCORE)))
    R = res.results
    y = np.concatenate([R[c]["y"] for c in range(NCORE)], axis=0)[None]
    ys = np.concatenate([R[c]["ys"] for c in range(NCORE)], axis=0)[:, None, :]
    klast = np.ascontiguousarray(R[NCORE - 1]["klast"].T).reshape(1, 1, 128, 2, 64)
    vlast = R[NCORE - 1]["vlast"].reshape(1, 1, 128, 2, 64)
    ulast = R[NCORE - 1]["ulast"][1:16].reshape(1, 1, 15, 512)
    nks = np.concatenate([R[c]["nks"] for c in range(NCORE)], axis=0).reshape(1, 128, 128, 2, 64)
    nvs = np.concatenate([R[c]["nvs"] for c in range(NCORE)], axis=0).reshape(1, 128, 128, 2, 64)
    nps = np.concatenate([R[c]["nps"] for c in range(NCORE)], axis=0).reshape(1, 128, 15, 512)
    return (y.astype(np.float32), ys.astype(np.float32), klast.astype(np.float32), vlast.astype(np.float32),
            ulast.astype(np.float32), nks.astype(np.float32), nvs.astype(np.float32), nps.astype(np.float32))
```

```python
import os
import numpy as np
from contextlib import ExitStack

import concourse.bass as bass
import concourse.mybir as mybir
from concourse.bass_utils import run_bass_kernel_spmd

F32 = mybir.dt.float32
BF16 = mybir.dt.bfloat16
AF = mybir.ActivationFunctionType
ALU = mybir.AluOpType
AX = mybir.AxisListType

NCORE = 8
DM = 1024
SEQ = 16384
TOK = SEQ // NCORE
TT = 512
NTILE = TOK // TT
NSMP = 128 // NCORE
NCOL = 2304
EPS = 1e-6
POOLW = (2, 4, 8, 16)

C_ID, C_MP, C_MC, C_MF, C_BLK = 0, 128, 256, 384, 512
NCB = 640
C_QW, C_KW, C_NW, C_PS, C_SK, C_RC, C_DM, C_EPS, C_M1 = 640, 641, 642, 650, 654, 658, 722, 786, 787
NCST = 788


class _Op:
    __slots__ = ("eng", "fn", "r", "w", "dma", "deps", "waits", "inc", "val", "seq", "c", "tbl", "t0", "t1", "nb", "grp")

    def __init__(self, eng, fn, r, w, dma, c, tbl, nb=0):
        self.eng, self.fn, self.r, self.w, self.dma = eng, fn, r, w, dma
        self.c, self.tbl, self.nb = c, tbl, nb
        self.deps = ()
        self.waits = []
        self.inc = False
        self.val = 0
        self.seq = 0
        self.t0 = self.t1 = 0.0


class Sched:
    ENG = ("pe", "act", "dve", "pool", "sp")
    DEFC = {"pe": 0.27, "act": 0.7, "dve": 0.6, "pool": 1.0, "sp": 3.0}
    ISSUE = {"sp": 0.08, "pool": 0.6, "act": 0.1}
    TBL_SWITCH = 1.4
    DMA_BW = 330e3
    XLAT = 0.15
    GRP_SWITCH = 0.1

    def __init__(self, nc):
        self.nc = nc
        self.ops = []
        self.grp = 0

    def add(self, eng, fn, r=(), w=(), dma=None, c=None, tbl=0, nb=0, pm=0):
        if c is None:
            c = self.DEFC["sp"] if dma is not None else self.DEFC[eng]
        op = _Op(eng, fn, tuple(r), tuple(w), dma, c, tbl, nb)
        op.grp = pm
        self.ops.append(op)

    def newgrp(self):
        self.grp += 1

    def _deps(self):
        ops = self.ops
        last_w, readers = {}, {}
        for i, op in enumerate(ops):
            deps = set()
            for r in op.r:
                if r in last_w:
                    deps.add(last_w[r])
            for w in op.w:
                if w in last_w:
                    deps.add(last_w[w])
                deps.update(readers.get(w, ()))
            deps.discard(i)
            op.deps = deps
            for r in op.r:
                readers.setdefault(r, []).append(i)
            for w in op.w:
                last_w[w] = i
                readers[w] = []

    def _list_schedule(self):
        ops = self.ops
        n = len(ops)
        succ = [[] for _ in range(n)]
        indeg = [0] * n
        for i, op in enumerate(ops):
            indeg[i] = len(op.deps)
            for j in op.deps:
                succ[j].append(i)
        est = [0.0] * n
        ready = {e: [] for e in self.ENG}
        for i, op in enumerate(ops):
            if indeg[i] == 0:
                ready[op.eng].append(i)
        free = {e: 0.0 for e in self.ENG}
        cur_tbl = 0
        cur_grp = -1
        dma_free = 0.0
        order = []
        done = 0
        while done < n:
            best = None
            for e in self.ENG:
                if not ready[e]:
                    continue
                fe = free[e]
                bi, bk = None, None
                for i in ready[e]:
                    stt = est[i] if est[i] > fe else fe
                    if e == "act" and ops[i].tbl and cur_tbl and ops[i].tbl != cur_tbl:
                        stt += self.TBL_SWITCH
                    if e == "pe" and ops[i].grp != cur_grp:
                        stt += self.GRP_SWITCH
                    k = (stt, i)
                    if bk is None or k < bk:
                        bk, bi = k, i
                if best is None or bk < best[0]:
                    best = (bk, e, bi)
            (stt, _), e, i = best
            op = ops[i]
            stt = max(est[i], free[e])
            if e == "act" and op.tbl and cur_tbl and op.tbl != cur_tbl:
                stt += self.TBL_SWITCH
            if e == "pe" and op.grp != cur_grp:
                stt += self.GRP_SWITCH
            ready[e].remove(i)
            if op.dma is not None:
                busy = self.ISSUE.get(e, 0.1)
                if op.nb:
                    beg = max(stt + busy + 1.5, dma_free)
                    dma_free = beg + op.nb / self.DMA_BW
                    op.t0, op.t1 = stt, dma_free + 0.5
                else:
                    op.t0, op.t1 = stt, stt + busy + op.c
                free[e] = stt + busy
            else:
                op.t0, op.t1 = stt, stt + op.c
                free[e] = op.t1
                if e == "act" and op.tbl:
                    cur_tbl = op.tbl
                if e == "pe":
                    cur_grp = op.grp
            order.append(i)
            done += 1
            for s_ in succ[i]:
                t_ = op.t1 + (self.XLAT if ops[s_].eng != e else 0.05)
                if t_ > est[s_]:
                    est[s_] = t_
                indeg[s_] -= 1
                if indeg[s_] == 0:
                    ready[ops[s_].eng].append(s_)
        self.sim_end = max(op.t1 for op in ops)
        return order

    def _resolve(self, reorder=True):
        self._deps()
        ops = self.ops
        if reorder:
            order = self._list_schedule()
            remap = {old: new for new, old in enumerate(order)}
            new_ops = [ops[i] for i in order]
            for op in new_ops:
                op.deps = {remap[j] for j in op.deps}
            self.ops = ops = new_ops
        dom, cnt = [], {}
        for op in ops:
            d = ("dma", op.dma) if op.dma is not None else op.eng
            cnt[d] = cnt.get(d, 0) + 1
            op.seq = cnt[d]
            dom.append(d)
        need = []
        for i, op in enumerate(ops):
            per = {}
            for j in op.deps:
                assert j < i
                d = dom[j]
                if d == "pe" and op.eng == "pe" and op.dma is None:
                    continue
                if d not in per or ops[j].seq > ops[per[d]].seq:
                    per[d] = j
            need.append(per)
            for d, j in per.items():
                ops[j].inc = True
        ctr = {}
        for i, op in enumerate(ops):
            d = dom[i]
            if op.dma is not None:
                ctr[d] = ctr.get(d, 0) + 16
                op.val = ctr[d]
                op.inc = True
            elif op.inc:
                ctr[d] = ctr.get(d, 0) + 1
                op.val = ctr[d]
        waited = {e: {} for e in self.ENG}
        known_at_issue = [None] * len(ops)
        for i, op in enumerate(ops):
            K = waited[op.eng]
            wl = []
            for d, j in sorted(need[i].items(), key=lambda t: -t[1]):
                v = ops[j].val
                if K.get(d, 0) >= v:
                    continue
                K[d] = v
                wl.append((d, v))
                for d2, v2 in known_at_issue[j].items():
                    if K.get(d2, 0) < v2:
                        K[d2] = v2
            op.waits = wl
            known_at_issue[i] = dict(K)
        self.dom = dom
        return set(dom)

    def emit(self, final_wait_eng="sp"):
        nc = self.nc
        doms = self._resolve()
        last = {}
        for i, op in enumerate(self.ops):
            if op.dma is not None:
                last[self.dom[i]] = op.val
        with ExitStack() as es:
            sems = {}
            for d in sorted(doms, key=str):
                nm = "s_" + (d if isinstance(d, str) else "d_" + d[1])
                sems[d] = es.enter_context(nc.semaphore(nm))
            block = es.enter_context(nc.Block())
            by_eng = {e: [] for e in self.ENG}
            for i, op in enumerate(self.ops):
                by_eng[op.eng].append(i)

            def run(e, handle, idxs):
                for i in idxs:
                    op = self.ops[i]
                    for d, v in op.waits:
                        handle.wait_ge(sems[d], v)
                    ins = op.fn(handle)
                    if op.inc:
                        ins.then_inc(sems[self.dom[i]], 16 if op.dma is not None else 1)
                if e == final_wait_eng:
                    for d, v in last.items():
                        handle.wait_ge(sems[d], v)

            @block.tensor
            def _(h):
                run("pe", h, by_eng["pe"])

            @block.scalar
            def _(h):
                run("act", h, by_eng["act"])

            @block.vector
            def _(h):
                run("dve", h, by_eng["dve"])

            @block.gpsimd
            def _(h):
                run("pool", h, by_eng["pool"])

            @block.sync
            def _(h):
                run("sp", h, by_eng["sp"])


_CACHE = {}


def _merge(a, b):
    items = [((i + 0.5) / len(a), 0, i, f) for i, f in enumerate(a)]
    items += [((i + 0.5) / len(b), 1, i, f) for i, f in enumerate(b)]
    items.sort(key=lambda t: (t[0], t[1], t[2]))
    return [t[3] for t in items]


def build_program():
    nc = bass.Bass("TRN2", target_bir_lowering=False)

    def din(name, shape):
        return nc.dram_tensor(name, list(shape), F32, kind="ExternalInput").ap()

    def dout(name, shape):
        return nc.dram_tensor(name, list(shape), F32, kind="ExternalOutput").ap()

    xh = din("xh", [TOK + 128, DM])
    xs = din("xs", [NSMP, DM])
    ck = din("ck", [NSMP, 128, 128])
    cv = din("cv", [NSMP, 128, 128])
    sp_in = din("sp", [NSMP, 15, 512])
    w_in = din("w_in", [DM, NCOL])
    w_out = din("w_out", [DM, DM])
    w_pool = din("w_pool", [4, 128, 128])
    cst_d = din("cst", [128, NCST])
    nwb_d = din("nwb", [128, DM])

    y_d = dout("y", [TOK, DM])
    ys_d = dout("ys", [NSMP, DM])
    kl_d = dout("klast", [128, 128])
    vl_d = dout("vlast", [128, 128])
    ul_d = dout("ulast", [16, 512])
    nks_d = dout("nks", [NSMP, 128, 128])
    nvs_d = dout("nvs", [NSMP, 128, 128])
    nps_d = dout("nps", [NSMP, 15, 512])

    es = ExitStack()
    with es:
        def sb(name, shape, dt=F32):
            return es.enter_context(nc.sbuf_tensor(name, list(shape), dt))

        win = sb("win", [128, 8, NCOL], BF16)
        wout = sb("wout", [128, 8, DM], BF16)
        wpool = sb("wpool", [128, 4, 128], BF16)
        sga = sb("sga", [128, 4, TT])
        sgp = sb("sgp", [128, 4, TT])
        cst = sb("cstf", [128, NCST])
        cstb = sb("cstb", [128, NCB], BF16)
        nwb = sb("nwb_s", [128, DM])
        onesb = sb("onesb", [128, 128], BF16)
        esink = sb("esink", [128, 4])
        xb = [sb(f"xb{i}", [128, DM]) for i in range(3)]
        xr = [sb(f"xr{i}", [128, DM]) for i in range(2)]
        hn = [sb(f"hn{i}", [128, DM], BF16) for i in range(2)]
        st = [sb(f"st{i}", [128, 4]) for i in range(3)]
        hT = [sb(f"hT{i}", [128, 8, TT], BF16) for i in range(2)] + [sb("hT2", [128, 8, NSMP], BF16)]
        sq = sb("sq", [128, TT], BF16)
        lnb = sb("lnb", [128, TT])
        qTn = [sb(f"qTn{i}", [128, 4, TT], BF16) for i in range(2)] + [sb("qTn2", [128, 4, NSMP], BF16)]
        kTn = sb("kTn", [128, 128 + TOK], BF16)
        kTf = sb("kTf", [128, 128])
        Vall = sb("Vall", [128, TOK // 128 + 1, 128], BF16)
        vTf = sb("vTf", [128, TT])
        vlf = sb("vlf", [128, 128])
        uT = sb("uT", [128, 4, 16 + TT])
        pta = sb("pta", [128, 16 + TT])
        ptb = sb("ptb", [128, 16 + TT])
        dT = sb("dT", [128, 4, TT], BF16)
        dfix = sb("dfix", [128, 16])
        PT = [sb(f"PT{i}", [128, 2, 2, 128], BF16) for i in range(4)]
        rden = sb("rden", [128, TT])
        oT = sb("oT", [128, TT])
        mixT = sb("mixT", [128, 8, TT], BF16)
        ysb = [sb(f"ysb{i}", [128, DM]) for i in range(2)]
        ckb = sb("ckb", [128, NSMP, 128], BF16)
        cvb = sb("cvb", [128, NSMP, 128], BF16)
        KT = sb("KT", [128, NSMP, 128], BF16)
        spf = sb("spf", [120, 2, 512])
        uh = sb("uh", [128, 4, 2, 120])
        hs = sb("hs", [128, 4, NSMP])
        PTs = sb("PTs", [128, 2, NSMP, 4], BF16)
        pnf = sb("pnf", [16, 2, 4, NSMP])
        pnb = sb("pnb", [16, 2, 4, NSMP], BF16)
        vsf = sb("vsf", [16, 128])
        vsb = sb("vsb", [16, 128], BF16)
        ksf = sb("ksf", [16, 128])
        usf = sb("usf", [16, 512])
        rdens = sb("rdens", [128, NSMP, 4])
        oTs = sb("oTs", [128, NSMP, 4])
        fz = sb("fz", [128, 8])
        kTs = sb("kTs", [128, NSMP], BF16)
        kTfs = sb("kTfs", [128, NSMP])
        vTfs = sb("vTfs", [128, NSMP])
        sqs = sb("sqs", [128, NSMP], BF16)
        lnbs = sb("lnbs", [128, NSMP])
        sgas = sb("sgas", [128, 4, NSMP])
        sgps = sb("sgps", [128, 4, NSMP])
        uTs = sb("uTs", [128, 4, NSMP])
        dTs = sb("dTs", [128, 4, NSMP], BF16)
        mixTs = sb("mixTs", [128, 8, NSMP], BF16)

        B = [es.enter_context(nc.psum_tensor(f"B{i}", [128, 512], F32)) for i in range(8)]
        B3h = B[3][:].bitcast(BF16)
        B4h = B[4][:].bitcast(BF16)

        identb = cstb[:, C_ID:C_ID + 128]
        identf = cst[:, C_ID:C_ID + 128]
        blkb = cstb[:, C_BLK:C_BLK + 128]
        mask_pc = cstb[:, C_MP:C_MP + 256].rearrange("p (m q) -> p m q", m=2)
        mask_f = cstb[:, C_MF:C_MF + 128]
        mask_c = cstb[:, C_MC:C_MC + 128]
        epsc = cst[:, C_EPS:C_EPS + 1]

        S = Sched(nc)
        CP = lambda n: n / 2350.0 + 0.005
        CA = lambda n: (n + 130) / 1200.0
        CD = lambda n: n / 960.0 + 0.12

        S.add("sp", lambda e: e.dma_start(out=cst[:], in_=cst_d), w=["cst"], dma="cst")
        S.add("sp", lambda e: e.dma_start(out=nwb[:], in_=nwb_d), w=["nwb"], dma="nwb")
        S.add("dve", lambda e: e.tensor_copy(out=cstb[:], in_=cst[:, 0:NCB]), r=["cst"], w=["cstb"])
        S.add("pool", lambda e: e.memset(onesb[:], 1.0), w=["onesb"])
        S.add("act", lambda e: e.activation(out=esink[:], in_=cst[:, C_SK:C_SK + 4], func=AF.Exp),
              r=["cst"], w=["esink"])

        w_in_v = w_in.rearrange("(k p) c -> p k c", p=128)
        stg = [sga[:].rearrange("p c t -> p (c t)"), sgp[:].rearrange("p c t -> p (c t)")]
        stg_res = ["sga", "sgp"]
        WGROUPS = [(0, 2), (2, 4), (4, 6), (6, 8), (8, 10), (10, 12), (12, 14), (14, 16), (16, 18)]
        wg_of = {}
        nstg = [0]

        def stage_load(dst_ap3, src_ap3, nk, ncols, res_w, cast_eng, stg=stg, stg_res=stg_res, tag="stg"):
            i = nstg[0] % len(stg)
            nstg[0] += 1
            sv = stg[i][:, 0:nk * ncols].rearrange("p (k c) -> p k c", k=nk)
            S.add("sp", lambda e: e.dma_start(out=sv, in_=src_ap3), w=[stg_res[i]], dma=stg_res[i], nb=128 * nk * ncols * 4)
            if cast_eng == "act":
                S.add("act", lambda e: e.activation(out=dst_ap3, in_=sv, func=AF.Copy), r=[stg_res[i]], w=[res_w],
                      c=CA(nk * ncols))
            else:
                S.add("dve", lambda e: e.tensor_copy(out=dst_ap3, in_=sv), r=[stg_res[i]], w=[res_w],
                      c=nk * ncols / 1900.0 + 0.1)

        def load_w_in():
            for gi, (j0, j1) in enumerate(WGROUPS):
                for j in range(j0, j1):
                    wg_of[j] = f"wing{gi}"
                for kh in range(2):
                    S.add("pool", lambda e, j0=j0, j1=j1, kh=kh: e.dma_start(
                        out=win[:, 4 * kh:4 * kh + 4, j0 * 128:j1 * 128],
                        in_=w_in_v[:, 4 * kh:4 * kh + 4, j0 * 128:j1 * 128]),
                        w=[f"wing{gi}_{kh}"], dma=f"wing{gi}_{kh}", c=3.0 + gi * 5.0 + kh * 2.5)

        def load_late_weights():
            w_out_v = w_out.rearrange("(m p) c -> p m c", p=128)
            stgB = [ysb[0][:], ysb[1][:], xr[0][:]]
            resB = ["ysb0", "ysb1", "xr0"]
            gate = [f"wing{len(WGROUPS) - 1}_1"]
            for m in range(8):
                S.add("pool", lambda e, m=m: e.dma_start(out=wout[:, m:m + 1, :], in_=w_out_v[:, m:m + 1, :]),
                      r=gate, w=[f"wout{m}"], dma=f"wout{m}", c=50.0 + 3.0 * m)
            S.add("pool", lambda e: e.dma_start(out=wpool[:], in_=w_pool.rearrange("g c d -> c g d")),
                  r=gate, w=["wpool"], dma="wpool", c=45.0)
            for q4 in range(4):
                bs = slice(4 * q4, 4 * q4 + 4)
                S.add("pool", lambda e, bs=bs: e.dma_start(out=ckb[:, bs, :], in_=ck[bs].rearrange("b j d -> j b d")),
                      r=["wout7"], w=["ckb"], dma="ckb", nb=1 << 20)
                S.add("pool", lambda e, bs=bs: e.dma_start(out=cvb[:, bs, :], in_=cv[bs].rearrange("b j d -> j b d")),
                      r=["wout7"], w=["cvb"], dma="cvb", nb=1 << 20)

        cnt = {"xb": 0, "hn": 0, "xr": 0, "ysb": 0, "zb": 0}

        def fence(tail_ap, rows, res, col):
            S.add("dve", lambda e: e.tensor_copy(out=fz[0:rows, col:col + 1], in_=tail_ap), r=[res], w=[f"fz{col}"], c=0.08)
            return f"fz{col}"

        def x_stage(src_ap, rows, hT_slot, col0):
            S.newgrp()
            xs_ = cnt["xb"] % 3
            cnt["xb"] += 1
            hs_ = cnt["hn"] % 2
            cnt["hn"] += 1
            X, H, ST = xb[xs_], hn[hs_], st[xs_]
            rx, rh, rst = f"xb{xs_}", f"hn{hs_}", f"st{xs_}"
            S.add("sp", lambda e: e.dma_start(out=X[0:rows, :], in_=src_ap), w=[rx], dma=rx, nb=rows * DM * 4)
            S.add("act", lambda e: e.activation(out=H[0:rows, :], in_=X[0:rows, :], func=AF.Square,
                                                accum_out=ST[0:rows, 0:1]), r=[rx], w=[rh, rst], c=CA(DM))
            S.add("act", lambda e: e.activation(out=ST[0:rows, 1:2], in_=ST[0:rows, 0:1], func=AF.Ln,
                                                scale=1.0 / DM, bias=epsc[0:rows]), r=[rst, "cst"], w=[rst], c=0.3, tbl=1)
            S.add("act", lambda e: e.activation(out=ST[0:rows, 2:3], in_=ST[0:rows, 1:2], func=AF.Exp,
                                                scale=-0.5), r=[rst], w=[rst], c=0.3, tbl=1)
            S.add("dve", lambda e: e.scalar_tensor_tensor(out=H[0:rows, :], in0=X[0:rows, :], scalar=ST[0:rows, 2:3],
                                                          in1=nwb[0:rows, :], op0=ALU.mult, op1=ALU.mult),
                  r=[rx, rst, "nwb"], w=[rh], c=CD(DM))
            for k in range(8):
                S.add("pe", lambda e, k=k: e.transpose(B3h[:, k * 128:k * 128 + rows],
                                                       H[0:rows, k * 128:(k + 1) * 128], identb[0:rows, 0:rows]),
                      r=[rh, "cstb"], w=["B3"], c=0.1)
            S.add("dve", lambda e: e.tensor_copy(
                out=hT[hT_slot][:, :, col0:col0 + rows],
                in_=B3h[:].rearrange("p (k t) -> p k t", k=8)[:, :, 0:rows]),
                r=["B3"], w=[f"hT{hT_slot}"], c=0.7)

        def inproj(j, hT_slot, T):
            S.newgrp()
            b_ = cnt["zb"] % 2
            cnt["zb"] += 1
            for k in range(8):
                S.add("pe", lambda e, k=k, b_=b_: e.matmul(B[b_][:, 0:T], win[:, k, j * 128:(j + 1) * 128],
                                                          hT[hT_slot][:, k, 0:T], start=(k == 0), stop=(k == 7)),
                      r=[f"{wg_of[j]}_{k // 4}", f"hT{hT_slot}"], w=[f"B{b_}"], c=CP(T))
            return B[b_], f"B{b_}"

        def qk_norm(bank, rb, T, wcol, out_ap, out_res, out_f32=None):
            sq_, rsq, lnb_, rln = (sq, "sq", lnb, "lnb") if T > NSMP else (sqs, "sqs", lnbs, "lnbs")
            _qk_norm(bank, rb, T, wcol, out_ap, out_res, out_f32, sq_, rsq, lnb_, rln)

        def _qk_norm(bank, rb, T, wcol, out_ap, out_res, out_f32, sq, rsq, lnb, rln):
            S.newgrp()
            S.add("act", lambda e: e.activation(out=sq[:, 0:T], in_=bank[:, 0:T], func=AF.Square), r=[rb], w=[rsq], c=CA(T))
            S.add("pe", lambda e: e.matmul(B[2][:, 0:T], blkb, sq[:, 0:T], start=True, stop=True),
                  r=[rsq, "cstb"], w=["B2"], c=CP(T))
            S.add("act", lambda e: e.activation(out=lnb[:, 0:T], in_=B[2][:, 0:T], func=AF.Ln, scale=1.0 / 64,
                                                bias=epsc), r=["B2", "cst"], w=[rln], c=CA(T), tbl=1)
            S.add("act", lambda e: e.activation(out=lnb[:, 0:T], in_=lnb[:, 0:T], func=AF.Exp, scale=-0.5),
                  r=[rln], w=[rln], c=CA(T), tbl=1)
            S.add("dve", lambda e: e.scalar_tensor_tensor(out=out_ap, in0=bank[:, 0:T], scalar=cst[:, wcol:wcol + 1],
                                                          in1=lnb[:, 0:T], op0=ALU.mult, op1=ALU.mult),
                  r=[rb, rln, "cst"], w=[out_res], c=CD(T))
            if out_f32 is not None:
                o_ap, o_res, c0, c1 = out_f32
                S.add("dve", lambda e: e.scalar_tensor_tensor(out=o_ap, in0=bank[:, c0:c1], scalar=cst[:, wcol:wcol + 1],
                                                              in1=lnb[:, c0:c1], op0=ALU.mult, op1=ALU.mult),
                      r=[rb, rln, "cst"], w=[o_res], c=0.25)

        def proj_k(hT_slot, T, kcol0, last_f32):
            rows = min(T, 128)
            bank, rb = inproj(0, hT_slot, T)
            if T == NSMP:
                qk_norm(bank, rb, T, C_KW, kTs[:, 0:T], "kTs", out_f32=(kTfs[:, 0:T], "kTfs", 0, T))
                return
            qk_norm(bank, rb, T, C_KW, kTn[:, kcol0:kcol0 + T], "kTn",
                    out_f32=(kTf[:, 0:rows], "kTf", T - rows, T) if last_f32 else None)

        def proj_q(c, hT_slot, T, qs):
            bank, rb = inproj(6 + c, hT_slot, T)
            qk_norm(bank, rb, T, C_QW, qTn[qs][:, c, 0:T], f"qTn{qs}")

        def proj_v(hT_slot, T, vblk0, last_f32):
            nblk = max(T // 128, 1)
            rows = min(T, 128)
            bank, rb = inproj(1, hT_slot, T)
            vT_, rvT = (vTf, "vTf") if T > NSMP else (vTfs, "vTfs")
            S.newgrp()
            S.add("act", lambda e: e.activation(out=vT_[:, 0:T], in_=bank[:, 0:T], func=AF.Copy), r=[rb], w=[rvT], c=CA(T))
            for bi in range(nblk):
                S.add("pe", lambda e, bi=bi: e.transpose(B[2][0:rows, bi * 128:(bi + 1) * 128],
                                                         vT_[:, bi * 128:bi * 128 + rows], identf),
                      r=[rvT, "cst"], w=["B2"], c=0.27)
            if T >= 128:
                S.add("dve", lambda e: e.tensor_copy(
                    out=Vall[:, vblk0:vblk0 + nblk, :],
                    in_=B[2][:, 0:nblk * 128].rearrange("p (b d) -> p b d", b=nblk)), r=["B2"], w=["Vall"], c=CD(T))
                if last_f32:
                    S.add("dve", lambda e: e.tensor_copy(out=vlf[:], in_=B[2][:, (nblk - 1) * 128:nblk * 128]),
                          r=["B2"], w=["vlf"], c=0.25)
            else:
                S.add("dve", lambda e: e.tensor_copy(out=vsf[:], in_=B[2][0:rows, 0:128]), r=["B2"], w=["vsf"], c=0.25)
                S.add("dve", lambda e: e.tensor_copy(out=vsb[:], in_=B[2][0:rows, 0:128]), r=["B2"], w=["vsb"], c=0.25)

        def proj_u(g, hT_slot, T, ucol0, halo=False):
            bank, rb = inproj(2 + g, hT_slot, T)
            if T == NSMP:
                S.add("act", lambda e: e.activation(out=uTs[:, g, 0:T], in_=bank[:, 0:T], func=AF.Copy),
                      r=[rb], w=[f"uTs{g}"], c=CA(T))
            elif not halo:
                S.add("act", lambda e: e.activation(out=uT[:, g, ucol0:ucol0 + T], in_=bank[:, 0:T], func=AF.Copy),
                      r=[rb], w=[f"uT{g}"], c=CA(T))
            else:
                S.add("act", lambda e: e.activation(out=uT[:, g, 0:16], in_=bank[:, T - 16:T], func=AF.Copy),
                      r=[rb], w=["uTh"], c=0.3)

        def proj_ga(c, hT_slot, T):
            bank, rb = inproj(10 + c, hT_slot, T)
            dst, rd = (sga, "sga") if T > NSMP else (sgas, "sgas")
            S.add("act", lambda e: e.activation(out=dst[:, c, 0:T], in_=bank[:, 0:T], func=AF.Silu), r=[rb], w=[rd], c=CA(T), tbl=2)

        def proj_gp(g, hT_slot, T):
            bank, rb = inproj(14 + g, hT_slot, T)
            dst, rd = (sgp, "sgp") if T > NSMP else (sgps, "sgps")
            S.add("act", lambda e: e.activation(out=dst[:, g, 0:T], in_=bank[:, 0:T], func=AF.Silu), r=[rb], w=[rd], c=CA(T), tbl=2)

        def attn_scores(bi, cp, kcol, first, qs):
            S.newgrp()
            q0 = bi * 128
            for h in range(2):
                hp = slice(64 * h, 64 * h + 64)
                sb_ = B[4 + h]
                for pc in range(2):
                    kc = kcol - 128 + 128 * pc
                    S.add("pe", lambda e, hp=hp, sb_=sb_, pc=pc, kc=kc, h=h: e.matmul(
                        sb_[:, pc * 256:(pc + 1) * 256].rearrange("p (c q) -> p c q", c=2),
                        kTn[hp, kc:kc + 128], qTn[qs][hp, 2 * cp:2 * cp + 2, q0:q0 + 128],
                        start=True, stop=True, tile_position=(64 * h, 0)),
                        r=["kTn", f"qTn{qs}"], w=[f"B{4 + h}"], c=0.15, pm=1)
                pt = PT[2 * cp + h]
                rpt = f"PT{2 * cp + h}"
                S.add("act", lambda e, sb_=sb_, pt=pt: e.activation(
                    out=pt[:].rearrange("p a c q -> p (a c q)"), in_=sb_[:], func=AF.Exp, scale=0.125),
                    r=[f"B{4 + h}"], w=[rpt], c=0.55, tbl=1)
                if first:
                    S.add("dve", lambda e, pt=pt: e.tensor_tensor(
                        out=pt[:, 0], in0=pt[:, 0], in1=mask_f.unsqueeze(1).to_broadcast([128, 2, 128]),
                        op=ALU.mult), r=[rpt, "cstb"], w=[rpt], c=0.3)
                    S.add("dve", lambda e, pt=pt: e.tensor_tensor(
                        out=pt[:, 1], in0=pt[:, 1], in1=mask_c.unsqueeze(1).to_broadcast([128, 2, 128]),
                        op=ALU.mult), r=[rpt, "cstb"], w=[rpt], c=0.3)
                else:
                    S.add("dve", lambda e, pt=pt: e.tensor_tensor(
                        out=pt[:], in0=pt[:], in1=mask_pc.unsqueeze(2).to_broadcast([128, 2, 2, 128]),
                        op=ALU.mult), r=[rpt, "cstb"], w=[rpt], c=0.45)

        def attn_pv(bi, cp, vblk):
            S.newgrp()
            for h in range(2):
                hp = slice(64 * h, 64 * h + 64)
                pt = PT[2 * cp + h]
                rpt = f"PT{2 * cp + h}"
                for pc in range(2):
                    S.add("pe", lambda e, hp=hp, pt=pt, pc=pc, h=h: e.matmul(
                        B[7][hp, cp * 256:(cp + 1) * 256].rearrange("p (c q) -> p c q", c=2),
                        onesb[:, 0:64], pt[:, pc], start=(pc == 0), stop=(pc == 1),
                        tile_position=(0, 64 * h)), r=[rpt, "onesb"], w=["B7"], c=0.12, pm=2)
                for ci in range(2):
                    c = 2 * cp + ci
                    for pc in range(2):
                        vb = vblk - 1 + pc
                        S.add("pe", lambda e, hp=hp, pt=pt, pc=pc, ci=ci, c=c, vb=vb, h=h: e.matmul(
                            B[6][hp, c * 128:(c + 1) * 128], Vall[:, vb, 64 * h:64 * h + 64], pt[:, pc, ci, :],
                            start=(pc == 0), stop=(pc == 1), tile_position=(0, 64 * h)),
                            r=[rpt, "Vall"], w=["B6"], c=0.06, pm=2)

        def attn_epi(bi):
            S.newgrp()
            q0 = bi * 128
            S.add("dve", lambda e: e.tensor_tensor(
                out=rden[:].rearrange("p (c q) -> p c q", c=4), in0=B[7][:].rearrange("p (c q) -> p c q", c=4),
                in1=esink[:].unsqueeze(2).to_broadcast([128, 4, 128]), op=ALU.add), r=["B7", "esink"], w=["rden"], c=0.65)
            S.add("act", lambda e: e.activation(out=rden[:], in_=rden[:], func=AF.Ln), r=["rden"], w=["rden"], c=0.6, tbl=1)
            S.add("act", lambda e: e.activation(out=rden[:], in_=rden[:], func=AF.Exp, scale=-1.0), r=["rden"], w=["rden"], c=0.6, tbl=1)
            S.add("dve", lambda e: e.tensor_tensor(out=oT[:], in0=B[6][:], in1=rden[:], op=ALU.mult),
                  r=["B6", "rden"], w=["oT"], c=0.65)
            S.add("pool", lambda e: e.tensor_tensor(
                out=mixT[:, 0:4, q0:q0 + 128], in0=oT[:].rearrange("p (c q) -> p c q", c=4),
                in1=sga[:, :, q0:q0 + 128], op=ALU.mult), r=["oT", "sga"], w=[f"mixTa{bi}"], c=1.3)

        def pool_group(g, T, first_tile, bank_i):
            w_ = POOLW[g]
            U = uT[:, g, :]
            src, src_res = U, None
            sh, bi, lo = 1, 0, 0
            bufs = [pta, ptb]
            while sh < w_:
                dst = bufs[bi]
                dres = "pta" if bi == 0 else "ptb"
                lo2 = lo + sh
                rr = [f"uT{g}", "uTh"] if src_res is None else [src_res]
                S.add("pool", lambda e, dst=dst, src=src, lo2=lo2, sh=sh: e.tensor_tensor(
                    out=dst[:, lo2:16 + T], in0=src[:, lo2:16 + T], in1=src[:, lo2 - sh:16 + T - sh], op=ALU.add),
                    r=rr, w=[dres], c=1.1)
                src, src_res = dst, dres
                lo = lo2
                sh *= 2
                bi ^= 1
            S.add("dve", lambda e, src=src: e.scalar_tensor_tensor(
                out=dT[:, g, 0:T], in0=src[:, 16:16 + T], scalar=1.0 / w_, in1=uT[:, g, 16:16 + T],
                op0=ALU.mult, op1=ALU.subtract), r=[src_res, f"uT{g}"], w=["dT"], c=CD(T))
            if first_tile:
                S.add("dve", lambda e, src=src: e.tensor_tensor(
                    out=dfix[:], in0=src[:, 16:32], in1=cst[:, C_RC + 16 * g:C_RC + 16 * g + 16], op=ALU.mult),
                    r=[src_res, "cst"], w=["dfix"], c=0.15)
                S.add("dve", lambda e: e.tensor_tensor(
                    out=dT[:, g, 0:16], in0=dfix[:], in1=uT[:, g, 16:32], op=ALU.subtract),
                    r=["dfix", f"uT{g}"], w=["dT"], c=0.15)
            pool_proj(g, T, bank_i)

        def pool_proj(g, T, bank_i):
            S.newgrp()
            dT_, rdT, mx, rmx, sg, rsg = ((dT, "dT", mixT, "mixTp", sgp, "sgp") if T > NSMP
                                          else (dTs, "dTs", mixTs, "mixTs", sgps, "sgps"))
            S.add("pe", lambda e: e.matmul(B[bank_i][:, 0:T], wpool[:, g, :], dT_[:, g, 0:T], start=True, stop=True),
                  r=["wpool", rdT], w=[f"B{bank_i}"], c=CP(T))
            S.add("dve", lambda e: e.scalar_tensor_tensor(
                out=mx[:, 4 + g, 0:T], in0=B[bank_i][:, 0:T], scalar=cst[:, C_PS + g:C_PS + g + 1],
                in1=sg[:, g, 0:T], op0=ALU.mult, op1=ALU.mult), r=[f"B{bank_i}", "cst", rsg], w=[rmx], c=CD(T))

        def out_block(col0, rows, x_src, y_dst, banks):
            S.newgrp()
            smp = rows == NSMP
            mx = mixTs if smp else mixT
            for nh in range(2):
                bk = banks[nh]
                for m in range(8):
                    rmx = "mixTs" if smp else (f"mixTa{col0 // 128}" if m < 4 else "mixTp")
                    S.add("pe", lambda e, nh=nh, m=m, bk=bk: e.matmul(
                        B[bk][0:rows, :], mx[:, m, col0:col0 + rows], wout[:, m, nh * 512:(nh + 1) * 512],
                        start=(m == 0), stop=(m == 7)),
                        r=[rmx, f"wout{m}"], w=[f"B{bk}"], c=0.225)
            rs_ = cnt["xr"] % 2
            cnt["xr"] += 1
            ys_ = cnt["ysb"] % 2
            cnt["ysb"] += 1
            XR, Y = xr[rs_], ysb[ys_]
            S.add("sp", lambda e: e.dma_start(out=XR[0:rows, :], in_=x_src), w=[f"xr{rs_}"], dma=f"xr{rs_}", nb=rows * DM * 4)
            for nh in range(2):
                bk = banks[nh]
                S.add("dve", lambda e, nh=nh, bk=bk: e.tensor_tensor(
                    out=Y[0:rows, nh * 512:(nh + 1) * 512], in0=B[bk][0:rows, :],
                    in1=XR[0:rows, nh * 512:(nh + 1) * 512], op=ALU.add),
                    r=[f"B{bk}", f"xr{rs_}"], w=[f"ysb{ys_}"], c=0.65)
            fr = fence(Y[0:rows, DM - 1:DM], rows, f"ysb{ys_}", ys_)
            S.add("sp", lambda e: e.dma_start(out=y_dst, in_=Y[0:rows, :]), r=[f"ysb{ys_}", fr], dma=f"ysb{ys_}", nb=rows * DM * 4)

        def B_steps(ti):
            hs_ = (ti + 1) % 2
            qs = ti % 2
            t0 = ti * TT
            last = ti == NTILE - 1
            X = []
            for bi in range(4):
                X.append(lambda bi=bi: x_stage(xh[128 + t0 + bi * 128:128 + t0 + (bi + 1) * 128, :], 128, hs_, bi * 128))
            G1 = []
            G1.append(lambda: proj_k(hs_, TT, 128 + t0, last))
            G1.append(lambda: proj_v(hs_, TT, 1 + ti * 4, last))
            for c in range(2):
                G1.append(lambda c=c: proj_q(c, hs_, TT, qs))
            if last:
                def _kv_out():
                    f1 = fence(kTf[:, 127:128], 128, "kTf", 2)
                    f2 = fence(vlf[:, 127:128], 128, "vlf", 3)
                    S.add("sp", lambda e: e.dma_start(out=kl_d, in_=kTf[:]), r=["kTf", f1], dma="kl")
                    S.add("sp", lambda e: e.dma_start(out=vl_d, in_=vlf[:]), r=["vlf", f2], dma="vl")
                G1.append(_kv_out)
            G2 = [lambda c=c: proj_q(c, hs_, TT, qs) for c in range(2, 4)]
            G2 += [lambda c=c: proj_ga(c, hs_, TT) for c in range(4)]
            G3 = []
            if ti > 0:
                G3.append(lambda: S.add("pool", lambda e: e.tensor_copy(out=uT[:, :, 0:16], in_=uT[:, :, TT:TT + 16]),
                                        r=[f"uT{g}" for g in range(4)], w=["uTh"]))
            for g in range(4):
                G3.append(lambda g=g: proj_u(g, hs_, TT, 16))
            for g in range(4):
                G3.append(lambda g=g: proj_gp(g, hs_, TT))
            return X, G1, G2, G3

        def A_steps(ti):
            qs = ti % 2
            t0 = ti * TT
            last = ti == NTILE - 1
            sc = []
            for s in range(8):
                bi, cp = s // 2, s % 2
                sc.append((bi, cp))
            P1 = []
            def SC(s):
                bi, cp = sc[s]
                return lambda: attn_scores(bi, cp, 128 + t0 + bi * 128, ti == 0 and bi == 0, qs)
            def DO(s):
                bi, cp = sc[s]
                return lambda: attn_pv(bi, cp, 1 + ti * 4 + bi)
            P1.append(SC(0))
            for s in range(8):
                if s + 1 < 8:
                    P1.append(SC(s + 1))
                P1.append(DO(s))
                if sc[s][1] == 1:
                    P1.append(lambda bi=sc[s][0]: attn_epi(bi))
                    if last:
                        bi = sc[s][0]
                        r0 = t0 + bi * 128
                        P1.append(lambda bi=bi, r0=r0: out_block(bi * 128, 128, xh[128 + r0:128 + r0 + 128, :],
                                                                 y_d[r0:r0 + 128, :], (0, 1) if bi % 2 == 0 else (2, 3)))
            P2 = [lambda g=g: pool_group(g, TT, ti == 0, (2 + (g % 2)) if last else (4 + (g % 2))) for g in range(4)]
            P3 = []
            for bi in range(4):
                r0 = t0 + bi * 128
                if not last:
                    P3.append(lambda bi=bi, r0=r0: out_block(bi * 128, 128, xh[128 + r0:128 + r0 + 128, :],
                                                             y_d[r0:r0 + 128, :], (6, 7) if bi % 2 == 0 else (4, 5)))
            if last:
                def _u_out():
                    for g in range(4):
                        S.add("pe", lambda e, g=g: e.transpose(B[2][0:16, g * 128:(g + 1) * 128],
                                                               uT[:, g, TT:TT + 16], identf),
                              r=[f"uT{g}", "cst"], w=["B2"])
                    S.add("dve", lambda e: e.tensor_copy(out=usf[:], in_=B[2][0:16, :]), r=["B2"], w=["usf"])
                    f3 = fence(usf[:, 511:512], 16, "usf", 4)
                    S.add("sp", lambda e: e.dma_start(out=ul_d, in_=usf[:]), r=["usf", f3], dma="ul")
                P2.append(_u_out)
            return P1, P2, P3

        def run(steps):
            for f in steps:
                S.newgrp()
                f()

        x_stage(xh[0:128, :], 128, 0, 0)
        load_w_in()
        proj_k(0, 128, 0, False)
        proj_v(0, 128, 0, False)
        for g in range(4):
            proj_u(g, 0, 128, 0, halo=True)
        BS = [B_steps(ti) for ti in range(NTILE)]
        X, G1, G2, G3 = BS[0]
        run(X)
        run(G1)
        load_late_weights()
        run(G2)
        run(_merge(G3, BS[1][0]))
        for ti in range(NTILE):
            P1, P2, P3 = A_steps(ti)
            if ti + 1 < NTILE:
                _, G1, G2, G3 = BS[ti + 1]
                if ti + 2 < NTILE:
                    G3 = G3 + BS[ti + 2][0]
                if ti + 2 == NTILE:
                    G3 = G3 + sample_steps(nc, S, locals())
                run(_merge(P1, G1))
                run(_merge(P2, G2))
                run(_merge(P3, G3))
            else:
                run(P2)
                run(P1)
                run(P3)

        S.emit()
        _CACHE["sched"] = S
    return nc


def sample_steps(nc, S, L):
    (xs, ck, cv, sp_in, ys_d, nks_d, nvs_d, nps_d) = (L[k] for k in
                                                    ("xs", "ck", "cv", "sp_in", "ys_d", "nks_d", "nvs_d", "nps_d"))
    B, cst, cstb, onesb, esink = L["B"], L["cst"], L["cstb"], L["onesb"], L["esink"]
    identf, identb, B3h = L["identf"], L["identb"], L["B3h"]
    ckb, cvb, KT, spf, uh, hs = L["ckb"], L["cvb"], L["KT"], L["spf"], L["uh"], L["hs"]
    PTs, pnf, pnb, vsf, vsb, ksf, usf = L["PTs"], L["pnf"], L["pnb"], L["vsf"], L["vsb"], L["ksf"], L["usf"]
    rdens, oTs = L["rdens"], L["oTs"]
    kTs, kTfs, uTs, dTs, mixTs, sgas = L["kTs"], L["kTfs"], L["uTs"], L["dTs"], L["mixTs"], L["sgas"]
    qTn = L["qTn"][2]
    N = NSMP
    SM = {"pe": 0.07, "act": 0.35, "dve": 0.25, "pool": 0.5, "sp": 3.0}

    def add(eng, fn, r=(), w=(), dma=None, c=None):
        if c is None:
            c = SM["sp"] if dma is not None else SM[eng]
        S.add(eng, fn, r=r, w=w, dma=dma, c=c, tbl=1 if eng == "act" else 0)

    steps = []

    def s_copies():
        add("sp", lambda e: e.dma_start(out=nks_d[:, 0:127, :], in_=ck[:, 1:128, :]), r=["ckb"], dma="nks_c")
        add("sp", lambda e: e.dma_start(out=nvs_d[:, 0:127, :], in_=cv[:, 1:128, :]), r=["cvb"], dma="nvs_c")
        add("sp", lambda e: e.dma_start(out=nps_d[:, 0:14, :], in_=sp_in[:, 1:15, :]), r=["cvb"], dma="nps_c")
        add("sp", lambda e: e.dma_start(out=spf[:], in_=sp_in.rearrange("(a b) t c -> (b t) a c", a=2)),
            r=["cvb"], w=["spf"], dma="spf")
    steps.append(s_copies)

    def s_kt(q8):
        def f():
            for i in range(8):
                b = q8 * 8 + i
                add("pe", lambda e, b=b, i=i: e.transpose(B3h[:, i * 128:(i + 1) * 128], ckb[:, b, :], identb),
                    r=["ckb", "cstb"], w=["B3"], c=0.08)
            add("dve", lambda e: e.tensor_copy(out=KT[:, q8 * 8:(q8 + 1) * 8, :],
                                               in_=B3h[:].rearrange("p (b j) -> p b j", b=8)),
                r=["B3"], w=["KT"], c=0.7)
        return f
    steps.append(s_kt(0))
    steps.append(s_kt(1))

    def s_hist(a):
        def f():
            for g in range(4):
                add("pe", lambda e, g=g: e.transpose(B[a][:, g * 128:g * 128 + 120],
                                                     spf[:, a, g * 128:(g + 1) * 128], identf[0:120, 0:120]),
                    r=["spf", "cst"], w=[f"B{a}"], c=0.27)
            add("dve", lambda e: e.tensor_copy(
                out=uh[:, :, a, :], in_=B[a][:].rearrange("p (g t) -> p g t", g=4)[:, :, 0:120]),
                r=[f"B{a}"], w=["uh"], c=0.6)
        return f
    steps.append(s_hist(0))
    steps.append(s_hist(1))

    steps.append(lambda: L["x_stage"](xs, N, 2, 0))
    steps.append(lambda: L["proj_k"](2, N, 0, True))
    for c in range(4):
        steps.append(lambda c=c: L["proj_q"](c, 2, N, 2))
    steps.append(lambda: L["proj_v"](2, N, 0, False))
    for g in range(4):
        steps.append(lambda g=g: L["proj_u"](g, 2, N, 0))
    for c in range(4):
        steps.append(lambda c=c: L["proj_ga"](c, 2, N))
    for g in range(4):
        steps.append(lambda g=g: L["proj_gp"](g, 2, N))

    def s_newrows():
        add("pe", lambda e: e.transpose(B[2][0:N, 0:128], kTfs[:, 0:N], identf), r=["kTfs", "cst"], w=["B2"], c=0.27)
        add("dve", lambda e: e.tensor_copy(out=ksf[:], in_=B[2][0:N, 0:128]), r=["B2"], w=["ksf"])
        f5 = L["fence"](ksf[:, 127:128], N, "ksf", 5)
        f6 = L["fence"](vsf[:, 127:128], N, "vsf", 6)
        add("sp", lambda e: e.dma_start(out=nks_d[:, 127, :], in_=ksf[:]), r=["ksf", f5], dma="nks_n")
        add("sp", lambda e: e.dma_start(out=nvs_d[:, 127, :], in_=vsf[:]), r=["vsf", f6], dma="nvs_n")
        for g in range(4):
            add("pe", lambda e, g=g: e.transpose(B[2][0:N, g * 128:(g + 1) * 128], uTs[:, g, 0:N], identf),
                r=[f"uTs{g}", "cst"], w=["B2"], c=0.27)
        add("dve", lambda e: e.tensor_copy(out=usf[:], in_=B[2][0:N, :]), r=["B2"], w=["usf"])
        f7 = L["fence"](usf[:, 511:512], N, "usf", 7)
        add("sp", lambda e: e.dma_start(out=nps_d[:, 14, :], in_=usf[:]), r=["usf", f7], dma="nps_n")
    steps.append(s_newrows)

    def s_scores(h):
        def f():
            hp = slice(64 * h, 64 * h + 64)
            bank = B[h]
            rb = f"B{h}"
            for b in range(N):
                add("pe", lambda e, b=b: e.matmul(
                    bank[:, b * 4:(b + 1) * 4], KT[hp, b, :], qTn[hp, :, b], start=True, stop=True,
                    tile_position=(64 * h, 0)), r=["KT", "qTn2"], w=[rb])
            add("pe", lambda e: e.matmul(
                bank[0:N, 64:128].rearrange("p (c b) -> p c b", c=4), kTs[hp, 0:N], qTn[hp, :, 0:N],
                start=True, stop=True, tile_position=(64 * h, 0)), r=["kTs", "qTn2"], w=[rb])
            add("act", lambda e: e.activation(
                out=PTs[:, h].rearrange("p b c -> p (b c)"), in_=bank[:, 0:64], func=AF.Exp, scale=0.125),
                r=[rb], w=[f"PTs{h}"])
            add("act", lambda e: e.activation(
                out=pnf[:, h].rearrange("p c b -> p (c b)"), in_=bank[0:N, 64:128], func=AF.Exp, scale=0.125),
                r=[rb], w=[f"pnf{h}"])
            add("dve", lambda e: e.tensor_tensor(
                out=pnb[:, h].rearrange("p c b -> p (c b)"), in0=pnf[:, h].rearrange("p c b -> p (c b)"),
                in1=cst[0:N, C_DM:C_DM + 64], op=ALU.mult), r=[f"pnf{h}", "cst"], w=[f"pnb{h}"])
        return f
    steps.append(s_scores(0))
    steps.append(s_scores(1))

    def s_pv(h):
        def f():
            hp = slice(64 * h, 64 * h + 64)
            add("pe", lambda e: e.matmul(
                B[2][hp, 0:64].rearrange("p (b c) -> p b c", b=N), onesb[0:N, 0:64],
                pnb[:, h].rearrange("p c b -> p b c"), start=True, stop=False, tile_position=(0, 64 * h),
                skip_group_check=True), r=[f"pnb{h}", "onesb"], w=["B2"])
            add("pe", lambda e: e.matmul(
                B[2][hp, 0:64], onesb[:, 0:64], PTs[:, h].rearrange("p b c -> p (b c)"),
                start=False, stop=True, tile_position=(0, 64 * h), skip_group_check=True),
                r=[f"PTs{h}", "onesb"], w=["B2"])
            add("pe", lambda e: e.matmul(
                B[3][hp, 0:64].rearrange("p (b c) -> p b c", b=N), vsb[:, 64 * h:64 * h + 64],
                pnb[:, h].rearrange("p c b -> p b c"), start=True, stop=False, tile_position=(0, 64 * h),
                skip_group_check=True), r=[f"pnb{h}", "vsb"], w=["B3"])
            for b in range(N):
                add("pe", lambda e, b=b: e.matmul(
                    B[3][hp, b * 4:(b + 1) * 4], cvb[:, b, 64 * h:64 * h + 64], PTs[:, h, b, :],
                    start=False, stop=(b == N - 1), tile_position=(0, 64 * h), skip_group_check=True),
                    r=[f"PTs{h}", "cvb"], w=["B3"])
        return f

    def s_epi():
        add("dve", lambda e: e.tensor_tensor(
            out=rdens[:], in0=B[2][:, 0:64].rearrange("p (b c) -> p b c", b=N),
            in1=esink[:].unsqueeze(1).to_broadcast([128, N, 4]), op=ALU.add), r=["B2", "esink"], w=["rdens"])
        add("dve", lambda e: e.reciprocal(out=rdens[:], in_=rdens[:]), r=["rdens"], w=["rdens"], c=0.5)
        add("dve", lambda e: e.tensor_tensor(out=oTs[:], in0=B[3][:, 0:64].rearrange("p (b c) -> p b c", b=N),
                                             in1=rdens[:], op=ALU.mult), r=["B3", "rdens"], w=["oTs"])
        add("dve", lambda e: e.tensor_tensor(out=mixTs[:, 0:4, 0:N], in0=oTs[:].rearrange("p b c -> p c b"),
                                             in1=sgas[:, :, 0:N], op=ALU.mult), r=["oTs", "sgas"], w=["mixTs"])
    def s_pv_epi():
        s_pv(0)()
        s_pv(1)()
        s_epi()
    steps.append(s_pv_epi)

    def s_pool(g):
        def f():
            w_ = POOLW[g]
            add("dve", lambda e: e.tensor_reduce(
                out=hs[:, g, :], in_=uh[:, g].rearrange("p a (b t) -> p (a b) t", t=15)[:, :, 16 - w_:15],
                axis=AX.X, op=ALU.add), r=["uh"], w=[f"hs{g}"])
            add("dve", lambda e: e.tensor_tensor(out=hs[:, g, :], in0=hs[:, g, :], in1=uTs[:, g, 0:N],
                                                 op=ALU.add), r=[f"hs{g}", f"uTs{g}"], w=[f"hs{g}"])
            add("dve", lambda e: e.scalar_tensor_tensor(
                out=dTs[:, g, 0:N], in0=hs[:, g, :], scalar=1.0 / w_, in1=uTs[:, g, 0:N],
                op0=ALU.mult, op1=ALU.subtract), r=[f"hs{g}", f"uTs{g}"], w=["dTs"])
            L["pool_proj"](g, N, g % 2)
        return f
    for g in range(4):
        steps.append(s_pool(g))
    steps.append(lambda: L["out_block"](0, N, xs, ys_d, (0, 1)))
    return steps


def _col_perm():
    p = list(range(512, 768))
    p += list(range(1280, 1792))
    for c in range(4):
        p += list(range(c * 64, c * 64 + 64)) + list(range((c + 4) * 64, (c + 4) * 64 + 64))
    for c in range(4):
        p += list(range(768 + c * 64, 768 + c * 64 + 64)) + list(range(768 + (c + 4) * 64, 768 + (c + 4) * 64 + 64))
    p += list(range(1792, 2304))
    return np.array(p)


def _row_perm():
    p = []
    for c in range(4):
        p += list(range(c * 64, c * 64 + 64)) + list(range((c + 4) * 64, (c + 4) * 64 + 64))
    p += list(range(512, 1024))
    return np.array(p)


def _consts(core, norm_w, q_norm_w, k_norm_w, sinks, pool_scale):
    c = np.zeros((128, NCST), np.float32)
    j = np.arange(128)[:, None]
    i = np.arange(128)[None, :]
    c[:, C_ID:C_ID + 128] = np.eye(128, dtype=np.float32)
    c[:, C_MC:C_MC + 128] = (j <= i)
    c[:, C_MP:C_MP + 128] = (j >= i)
    c[:, C_MF:C_MF + 128] = 0.0 if core == 0 else (j >= i)
    c[:, C_BLK:C_BLK + 128] = ((j // 64) == (i // 64))
    c[:, C_QW] = np.tile(q_norm_w, 2)
    c[:, C_KW] = np.tile(k_norm_w, 2)
    c[:, C_NW:C_NW + 8] = norm_w.reshape(8, 128).T
    c[:, C_PS:C_PS + 4] = pool_scale.reshape(4, 128).T
    for cc in range(4):
        c[0:64, C_SK + cc] = sinks[cc]
        c[64:128, C_SK + cc] = sinks[cc + 4]
    for g, w in enumerate(POOLW):
        pos = np.arange(16)
        cntv = np.minimum(w, pos + 1) if core == 0 else np.full(16, w)
        c[:, C_RC + 16 * g:C_RC + 16 * g + 16] = (1.0 / cntv)[None, :]
    dm = np.zeros((16, 4, 16), np.float32)
    for b in range(16):
        dm[b, :, b] = 1.0
    c[0:16, C_DM:C_DM + 64] = dm.reshape(16, 64)
    c[:, C_EPS] = EPS
    c[:, C_M1] = -1.0
    return c


def kernel(x_prompt, x_sample, cache_k, cache_v, state_pool, norm_w, w_in, q_norm_w, k_norm_w, sinks,
           w_pool, pool_scale, w_out):
    f = lambda a: np.ascontiguousarray(np.asarray(a, dtype=np.float32))
    xp = f(x_prompt)[0]
    xsm = f(x_sample)[:, 0, :]
    ckh = f(cache_k)[0].reshape(128, 128, 128)
    cvh = f(cache_v)[0].reshape(128, 128, 128)
    sph = f(state_pool)[0]
    w_in_p = np.ascontiguousarray(f(w_in)[0][:, _col_perm()])
    w_out_p = np.ascontiguousarray(f(w_out)[0][_row_perm(), :])
    w_pool_h = f(w_pool)[0]
    if "nc" not in _CACHE:
        _CACHE["nc"] = build_program()
    nc = _CACHE["nc"]
    nwb_h = np.ascontiguousarray(np.tile(f(norm_w)[0][None, :], (128, 1)))
    in_maps = []
    for c in range(NCORE):
        xhc = np.zeros((TOK + 128, DM), np.float32)
        lo = c * TOK
        if c > 0:
            xhc[0:128] = xp[lo - 128:lo]
        xhc[128:] = xp[lo:lo + TOK]
        in_maps.append({
            "xh": xhc,
            "xs": np.ascontiguousarray(xsm[c * NSMP:(c + 1) * NSMP]),
            "ck": np.ascontiguousarray(ckh[c * NSMP:(c + 1) * NSMP]),
            "cv": np.ascontiguousarray(cvh[c * NSMP:(c + 1) * NSMP]),
            "sp": np.ascontiguousarray(sph[c * NSMP:(c + 1) * NSMP]),
            "w_in": w_in_p, "w_out": w_out_p, "w_pool": w_pool_h,
            "cst": _consts(c, f(norm_w)[0], f(q_norm_w)[0], f(k_norm_w)[0], f(sinks)[0], f(pool_scale)[0]),
            "nwb": nwb_h,
        })
    res = run_bass_kernel_spmd(nc, in_maps, core_ids=list(range(NCORE)))
    R = res.results
    y = np.concatenate([R[c]["y"] for c in range(NCORE)], axis=0)[None]
    ys = np.concatenate([R[c]["ys"] for c in range(NCORE)], axis=0)[:, None, :]
    klast = np.ascontiguousarray(R[NCORE - 1]["klast"].T).reshape(1, 1, 128, 2, 64)
    vlast = R[NCORE - 1]["vlast"].reshape(1, 1, 128, 2, 64)
    ulast = R[NCORE - 1]["ulast"][1:16].reshape(1, 1, 15, 512)
    nks = np.concatenate([R[c]["nks"] for c in range(NCORE)], axis=0).reshape(1, 128, 128, 2, 64)
    nvs = np.concatenate([R[c]["nvs"] for c in range(NCORE)], axis=0).reshape(1, 128, 128, 2, 64)
    nps = np.concatenate([R[c]["nps"] for c in range(NCORE)], axis=0).reshape(1, 128, 15, 512)
    return (y.astype(np.float32), ys.astype(np.float32), klast.astype(np.float32), vlast.astype(np.float32),
            ulast.astype(np.float32), nks.astype(np.float32), nvs.astype(np.float32), nps.astype(np.float32))
```

```python
import os
import numpy as np
from contextlib import ExitStack

import concourse.bass as bass
import concourse.mybir as mybir
from concourse.bass_utils import run_bass_kernel_spmd

F32 = mybir.dt.float32
BF16 = mybir.dt.bfloat16
AF = mybir.ActivationFunctionType
ALU = mybir.AluOpType
AX = mybir.AxisListType

NCORE = 8
DM = 1024
SEQ = 16384
TOK = SEQ // NCORE
TT = 512
NTILE = TOK // TT
NSMP = 128 // NCORE
NCOL = 2304
EPS = 1e-6
POOLW = (2, 4, 8, 16)

C_ID, C_MP, C_MC, C_MF, C_BLK = 0, 128, 256, 384, 512
NCB = 640
C_QW, C_KW, C_NW, C_PS, C_SK, C_RC, C_DM, C_EPS, C_M1 = 640, 641, 642, 650, 654, 658, 722, 786, 787
NCST = 788


class _Op:
    __slots__ = ("eng", "fn", "r", "w", "dma", "deps", "waits", "inc", "val", "seq", "c", "tbl", "t0", "t1", "nb", "grp")

    def __init__(self, eng, fn, r, w, dma, c, tbl, nb=0):
        self.eng, self.fn, self.r, self.w, self.dma = eng, fn, r, w, dma
        self.c, self.tbl, self.nb = c, tbl, nb
        self.deps = ()
        self.waits = []
        self.inc = False
        self.val = 0
        self.seq = 0
        self.t0 = self.t1 = 0.0


class Sched:
    ENG = ("pe", "act", "dve", "pool", "sp")
    DEFC = {"pe": 0.27, "act": 0.7, "dve": 0.6, "pool": 1.0, "sp": 3.0}
    ISSUE = {"sp": 0.08, "pool": 0.6, "act": 0.1}
    TBL_SWITCH = 1.4
    DMA_BW = 330e3
    XLAT = 0.15
    GRP_SWITCH = 0.1

    def __init__(self, nc):
        self.nc = nc
        self.ops = []
        self.grp = 0

    def add(self, eng, fn, r=(), w=(), dma=None, c=None, tbl=0, nb=0, pm=0):
        if c is None:
            c = self.DEFC["sp"] if dma is not None else self.DEFC[eng]
        op = _Op(eng, fn, tuple(r), tuple(w), dma, c, tbl, nb)
        op.grp = pm
        self.ops.append(op)

    def newgrp(self):
        self.grp += 1

    def _deps(self):
        ops = self.ops
        last_w, readers = {}, {}
        for i, op in enumerate(ops):
            deps = set()
            for r in op.r:
                if r in last_w:
                    deps.add(last_w[r])
            for w in op.w:
                if w in last_w:
                    deps.add(last_w[w])
                deps.update(readers.get(w, ()))
            deps.discard(i)
            op.deps = deps
            for r in op.r:
                readers.setdefault(r, []).append(i)
            for w in op.w:
                last_w[w] = i
                readers[w] = []

    def _list_schedule(self):
        ops = self.ops
        n = len(ops)
        succ = [[] for _ in range(n)]
        indeg = [0] * n
        for i, op in enumerate(ops):
            indeg[i] = len(op.deps)
            for j in op.deps:
                succ[j].append(i)
        est = [0.0] * n
        ready = {e: [] for e in self.ENG}
        for i, op in enumerate(ops):
            if indeg[i] == 0:
                ready[op.eng].append(i)
        free = {e: 0.0 for e in self.ENG}
        cur_tbl = 0
        cur_grp = -1
        dma_free = 0.0
        order = []
        done = 0
        while done < n:
            best = None
            for e in self.ENG:
                if not ready[e]:
                    continue
                fe = free[e]
                bi, bk = None, None
                for i in ready[e]:
                    stt = est[i] if est[i] > fe else fe
                    if e == "act" and ops[i].tbl and cur_tbl and ops[i].tbl != cur_tbl:
                        stt += self.TBL_SWITCH
                    if e == "pe" and ops[i].grp != cur_grp:
                        stt += self.GRP_SWITCH
                    k = (stt, i)
                    if bk is None or k < bk:
                        bk, bi = k, i
                if best is None or bk < best[0]:
                    best = (bk, e, bi)
            (stt, _), e, i = best
            op = ops[i]
            stt = max(est[i], free[e])
            if e == "act" and op.tbl and cur_tbl and op.tbl != cur_tbl:
                stt += self.TBL_SWITCH
            if e == "pe" and op.grp != cur_grp:
                stt += self.GRP_SWITCH
            ready[e].remove(i)
            if op.dma is not None:
                busy = self.ISSUE.get(e, 0.1)
                if op.nb:
                    beg = max(stt + busy + 1.5, dma_free)
                    dma_free = beg + op.nb / self.DMA_BW
                    op.t0, op.t1 = stt, dma_free + 0.5
                else:
                    op.t0, op.t1 = stt, stt + busy + op.c
                free[e] = stt + busy
            else:
                op.t0, op.t1 = stt, stt + op.c
                free[e] = op.t1
                if e == "act" and op.tbl:
                    cur_tbl = op.tbl
                if e == "pe":
                    cur_grp = op.grp
            order.append(i)
            done += 1
            for s_ in succ[i]:
                t_ = op.t1 + (self.XLAT if ops[s_].eng != e else 0.05)
                if t_ > est[s_]:
                    est[s_] = t_
                indeg[s_] -= 1
                if indeg[s_] == 0:
                    ready[ops[s_].eng].append(s_)
        self.sim_end = max(op.t1 for op in ops)
        return order

    def _resolve(self, reorder=True):
        self._deps()
        ops = self.ops
        if reorder:
            order = self._list_schedule()
            remap = {old: new for new, old in enumerate(order)}
            new_ops = [ops[i] for i in order]
            for op in new_ops:
                op.deps = {remap[j] for j in op.deps}
            self.ops = ops = new_ops
        dom, cnt = [], {}
        for op in ops:
            d = ("dma", op.dma) if op.dma is not None else op.eng
            cnt[d] = cnt.get(d, 0) + 1
            op.seq = cnt[d]
            dom.append(d)
        need = []
        for i, op in enumerate(ops):
            per = {}
            for j in op.deps:
                assert j < i
                d = dom[j]
                if d == "pe" and op.eng == "pe" and op.dma is None:
                    continue
                if d not in per or ops[j].seq > ops[per[d]].seq:
                    per[d] = j
            need.append(per)
            for d, j in per.items():
                ops[j].inc = True
        ctr = {}
        for i, op in enumerate(ops):
            d = dom[i]
            if op.dma is not None:
                ctr[d] = ctr.get(d, 0) + 16
                op.val = ctr[d]
                op.inc = True
            elif op.inc:
                ctr[d] = ctr.get(d, 0) + 1
                op.val = ctr[d]
        waited = {e: {} for e in self.ENG}
        known_at_issue = [None] * len(ops)
        for i, op in enumerate(ops):
            K = waited[op.eng]
            wl = []
            for d, j in sorted(need[i].items(), key=lambda t: -t[1]):
                v = ops[j].val
                if K.get(d, 0) >= v:
                    continue
                K[d] = v
                wl.append((d, v))
                for d2, v2 in known_at_issue[j].items():
                    if K.get(d2, 0) < v2:
                        K[d2] = v2
            op.waits = wl
            known_at_issue[i] = dict(K)
        self.dom = dom
        return set(dom)

    def emit(self, final_wait_eng="sp"):
        nc = self.nc
        doms = self._resolve()
        last = {}
        for i, op in enumerate(self.ops):
            if op.dma is not None:
                last[self.dom[i]] = op.val
        with ExitStack() as es:
            sems = {}
            for d in sorted(doms, key=str):
                nm = "s_" + (d if isinstance(d, str) else "d_" + d[1])
                sems[d] = es.enter_context(nc.semaphore(nm))
            block = es.enter_context(nc.Block())
            by_eng = {e: [] for e in self.ENG}
            for i, op in enumerate(self.ops):
                by_eng[op.eng].append(i)

            def run(e, handle, idxs):
                for i in idxs:
                    op = self.ops[i]
                    for d, v in op.waits:
                        handle.wait_ge(sems[d], v)
                    ins = op.fn(handle)
                    if op.inc:
                        ins.then_inc(sems[self.dom[i]], 16 if op.dma is not None else 1)
                if e == final_wait_eng:
                    for d, v in last.items():
                        handle.wait_ge(sems[d], v)

            @block.tensor
            def _(h):
                run("pe", h, by_eng["pe"])

            @block.scalar
            def _(h):
                run("act", h, by_eng["act"])

            @block.vector
            def _(h):
                run("dve", h, by_eng["dve"])

            @block.gpsimd
            def _(h):
                run("pool", h, by_eng["pool"])

            @block.sync
            def _(h):
                run("sp", h, by_eng["sp"])


_CACHE = {}


def _merge(a, b):
    items = [((i + 0.5) / len(a), 0, i, f) for i, f in enumerate(a)]
    items += [((i + 0.5) / len(b), 1, i, f) for i, f in enumerate(b)]
    items.sort(key=lambda t: (t[0], t[1], t[2]))
    return [t[3] for t in items]


def build_program():
    nc = bass.Bass("TRN2", target_bir_lowering=False)

    def din(name, shape):
        return nc.dram_tensor(name, list(shape), F32, kind="ExternalInput").ap()

    def dout(name, shape):
        return nc.dram_tensor(name, list(shape), F32, kind="ExternalOutput").ap()

    xh = din("xh", [TOK + 128, DM])
    xs = din("xs", [NSMP, DM])
    ck = din("ck", [NSMP, 128, 128])
    cv = din("cv", [NSMP, 128, 128])
    sp_in = din("sp", [NSMP, 15, 512])
    w_in = din("w_in", [DM, NCOL])
    w_out = din("w_out", [DM, DM])
    w_pool = din("w_pool", [4, 128, 128])
    cst_d = din("cst", [128, NCST])
    nwb_d = din("nwb", [128, DM])

    y_d = dout("y", [TOK, DM])
    ys_d = dout("ys", [NSMP, DM])
    kl_d = dout("klast", [128, 128])
    vl_d = dout("vlast", [128, 128])
    ul_d = dout("ulast", [16, 512])
    nks_d = dout("nks", [NSMP, 128, 128])
    nvs_d = dout("nvs", [NSMP, 128, 128])
    nps_d = dout("nps", [NSMP, 15, 512])

    es = ExitStack()
    with es:
        def sb(name, shape, dt=F32):
            return es.enter_context(nc.sbuf_tensor(name, list(shape), dt))

        win = sb("win", [128, 8, NCOL], BF16)
        wout = sb("wout", [128, 8, DM], BF16)
        wpool = sb("wpool", [128, 4, 128], BF16)
        sga = sb("sga", [128, 4, TT])
        sgp = sb("sgp", [128, 4, TT])
        cst = sb("cstf", [128, NCST])
        cstb = sb("cstb", [128, NCB], BF16)
        nwb = sb("nwb_s", [128, DM])
        onesb = sb("onesb", [128, 128], BF16)
        esink = sb("esink", [128, 4])
        xb = [sb(f"xb{i}", [128, DM]) for i in range(3)]
        xr = [sb(f"xr{i}", [128, DM]) for i in range(2)]
        hn = [sb(f"hn{i}", [128, DM], BF16) for i in range(2)]
        st = [sb(f"st{i}", [128, 4]) for i in range(3)]
        hT = [sb(f"hT{i}", [128, 8, TT], BF16) for i in range(2)] + [sb("hT2", [128, 8, NSMP], BF16)]
        sq = sb("sq", [128, TT], BF16)
        lnb = sb("lnb", [128, TT])
        qTn = [sb(f"qTn{i}", [128, 4, TT], BF16) for i in range(2)] + [sb("qTn2", [128, 4, NSMP], BF16)]
        kTn = sb("kTn", [128, 128 + TOK], BF16)
        kTf = sb("kTf", [128, 128])
        Vall = sb("Vall", [128, TOK // 128 + 1, 128], BF16)
        vTf = sb("vTf", [128, TT])
        vlf = sb("vlf", [128, 128])
        uT = sb("uT", [128, 4, 16 + TT])
        pta = sb("pta", [128, 16 + TT])
        ptb = sb("ptb", [128, 16 + TT])
        dT = sb("dT", [128, 4, TT], BF16)
        dfix = sb("dfix", [128, 16])
        PT = [sb(f"PT{i}", [128, 2, 2, 128], BF16) for i in range(4)]
        rden = sb("rden", [128, TT])
        oT = sb("oT", [128, TT])
        mixT = sb("mixT", [128, 8, TT], BF16)
        ysb = [sb(f"ysb{i}", [128, DM]) for i in range(2)]
        ckb = sb("ckb", [128, NSMP, 128], BF16)
        cvb = sb("cvb", [128, NSMP, 128], BF16)
        KT = sb("KT", [128, NSMP, 128], BF16)
        spf = sb("spf", [120, 2, 512])
        uh = sb("uh", [128, 4, 2, 120])
        hs = sb("hs", [128, 4, NSMP])
        PTs = sb("PTs", [128, 2, NSMP, 4], BF16)
        pnf = sb("pnf", [16, 2, 4, NSMP])
        pnb = sb("pnb", [16, 2, 4, NSMP], BF16)
        vsf = sb("vsf", [16, 128])
        vsb = sb("vsb", [16, 128], BF16)
        ksf = sb("ksf", [16, 128])
        usf = sb("usf", [16, 512])
        rdens = sb("rdens", [128, NSMP, 4])
        oTs = sb("oTs", [128, NSMP, 4])
        fz = sb("fz", [128, 8])
        kTs = sb("kTs", [128, NSMP], BF16)
        kTfs = sb("kTfs", [128, NSMP])
        vTfs = sb("vTfs", [128, NSMP])
        sqs = sb("sqs", [128, NSMP], BF16)
        lnbs = sb("lnbs", [128, NSMP])
        sgas = sb("sgas", [128, 4, NSMP])
        sgps = sb("sgps", [128, 4, NSMP])
        uTs = sb("uTs", [128, 4, NSMP])
        dTs = sb("dTs", [128, 4, NSMP], BF16)
        mixTs = sb("mixTs", [128, 8, NSMP], BF16)

        B = [es.enter_context(nc.psum_tensor(f"B{i}", [128, 512], F32)) for i in range(8)]
        B3h = B[3][:].bitcast(BF16)
        B4h = B[4][:].bitcast(BF16)

        identb = cstb[:, C_ID:C_ID + 128]
        identf = cst[:, C_ID:C_ID + 128]
        blkb = cstb[:, C_BLK:C_BLK + 128]
        mask_pc = cstb[:, C_MP:C_MP + 256].rearrange("p (m q) -> p m q", m=2)
        mask_f = cstb[:, C_MF:C_MF + 128]
        mask_c = cstb[:, C_MC:C_MC + 128]
        epsc = cst[:, C_EPS:C_EPS + 1]

        S = Sched(nc)
        CP = lambda n: n / 2350.0 + 0.005
        CA = lambda n: (n + 130) / 1200.0
        CD = lambda n: n / 960.0 + 0.12

        S.add("sp", lambda e: e.dma_start(out=cst[:], in_=cst_d), w=["cst"], dma="cst")
        S.add("sp", lambda e: e.dma_start(out=nwb[:], in_=nwb_d), w=["nwb"], dma="nwb")
        S.add("dve", lambda e: e.tensor_copy(out=cstb[:], in_=cst[:, 0:NCB]), r=["cst"], w=["cstb"])
        S.add("pool", lambda e: e.memset(onesb[:], 1.0), w=["onesb"])
        S.add("act", lambda e: e.activation(out=esink[:], in_=cst[:, C_SK:C_SK + 4], func=AF.Exp),
              r=["cst"], w=["esink"])

        w_in_v = w_in.rearrange("(k p) c -> p k c", p=128)
        stg = [sga[:].rearrange("p c t -> p (c t)"), sgp[:].rearrange("p c t -> p (c t)")]
        stg_res = ["sga", "sgp"]
        WGROUPS = [(0, 2), (2, 4), (4, 6), (6, 8), (8, 10), (10, 12), (12, 14), (14, 16), (16, 18)]
        wg_of = {}
        nstg = [0]

        def stage_load(dst_ap3, src_ap3, nk, ncols, res_w, cast_eng, stg=stg, stg_res=stg_res, tag="stg"):
            i = nstg[0] % len(stg)
            nstg[0] += 1
            sv = stg[i][:, 0:nk * ncols].rearrange("p (k c) -> p k c", k=nk)
            S.add("sp", lambda e: e.dma_start(out=sv, in_=src_ap3), w=[stg_res[i]], dma=stg_res[i], nb=128 * nk * ncols * 4)
            if cast_eng == "act":
                S.add("act", lambda e: e.activation(out=dst_ap3, in_=sv, func=AF.Copy), r=[stg_res[i]], w=[res_w],
                      c=CA(nk * ncols))
            else:
                S.add("dve", lambda e: e.tensor_copy(out=dst_ap3, in_=sv), r=[stg_res[i]], w=[res_w],
                      c=nk * ncols / 1900.0 + 0.1)

        def load_w_in():
            for gi, (j0, j1) in enumerate(WGROUPS):
                for j in range(j0, j1):
                    wg_of[j] = f"wing{gi}"
                for kh in range(2):
                    S.add("pool", lambda e, j0=j0, j1=j1, kh=kh: e.dma_start(
                        out=win[:, 4 * kh:4 * kh + 4, j0 * 128:j1 * 128],
                        in_=w_in_v[:, 4 * kh:4 * kh + 4, j0 * 128:j1 * 128]),
                        w=[f"wing{gi}_{kh}"], dma=f"wing{gi}_{kh}", c=3.0 + gi * 5.0 + kh * 2.5)

        def load_late_weights():
            w_out_v = w_out.rearrange("(m p) c -> p m c", p=128)
            stgB = [ysb[0][:], ysb[1][:], xr[0][:]]
            resB = ["ysb0", "ysb1", "xr0"]
            gate = [f"wing{len(WGROUPS) - 1}_1"]
            for m in range(8):
                S.add("pool", lambda e, m=m: e.dma_start(out=wout[:, m:m + 1, :], in_=w_out_v[:, m:m + 1, :]),
                      r=gate, w=[f"wout{m}"], dma=f"wout{m}", c=50.0 + 3.0 * m)
            S.add("pool", lambda e: e.dma_start(out=wpool[:], in_=w_pool.rearrange("g c d -> c g d")),
                  r=gate, w=["wpool"], dma="wpool", c=45.0)
            for q4 in range(4):
                bs = slice(4 * q4, 4 * q4 + 4)
                S.add("pool", lambda e, bs=bs: e.dma_start(out=ckb[:, bs, :], in_=ck[bs].rearrange("b j d -> j b d")),
                      r=["wout7"], w=["ckb"], dma="ckb", nb=1 << 20)
                S.add("pool", lambda e, bs=bs: e.dma_start(out=cvb[:, bs, :], in_=cv[bs].rearrange("b j d -> j b d")),
                      r=["wout7"], w=["cvb"], dma="cvb", nb=1 << 20)

        cnt = {"xb": 0, "hn": 0, "xr": 0, "ysb": 0, "zb": 0}

        def fence(tail_ap, rows, res, col):
            S.add("dve", lambda e: e.tensor_copy(out=fz[0:rows, col:col + 1], in_=tail_ap), r=[res], w=[f"fz{col}"], c=0.08)
            return f"fz{col}"

        def x_stage(src_ap, rows, hT_slot, col0):
            S.newgrp()
            xs_ = cnt["xb"] % 3
            cnt["xb"] += 1
            hs_ = cnt["hn"] % 2
            cnt["hn"] += 1
            X, H, ST = xb[xs_], hn[hs_], st[xs_]
            rx, rh, rst = f"xb{xs_}", f"hn{hs_}", f"st{xs_}"
            S.add("sp", lambda e: e.dma_start(out=X[0:rows, :], in_=src_ap), w=[rx], dma=rx, nb=rows * DM * 4)
            S.add("act", lambda e: e.activation(out=H[0:rows, :], in_=X[0:rows, :], func=AF.Square,
                                                accum_out=ST[0:rows, 0:1]), r=[rx], w=[rh, rst], c=CA(DM))
            S.add("act", lambda e: e.activation(out=ST[0:rows, 1:2], in_=ST[0:rows, 0:1], func=AF.Ln,
                                                scale=1.0 / DM, bias=epsc[0:rows]), r=[rst, "cst"], w=[rst], c=0.3, tbl=1)
            S.add("act", lambda e: e.activation(out=ST[0:rows, 2:3], in_=ST[0:rows, 1:2], func=AF.Exp,
                                                scale=-0.5), r=[rst], w=[rst], c=0.3, tbl=1)
            S.add("dve", lambda e: e.scalar_tensor_tensor(out=H[0:rows, :], in0=X[0:rows, :], scalar=ST[0:rows, 2:3],
                                                          in1=nwb[0:rows, :], op0=ALU.mult, op1=ALU.mult),
                  r=[rx, rst, "nwb"], w=[rh], c=CD(DM))
            for k in range(8):
                S.add("pe", lambda e, k=k: e.transpose(B3h[:, k * 128:k * 128 + rows],
                                                       H[0:rows, k * 128:(k + 1) * 128], identb[0:rows, 0:rows]),
                      r=[rh, "cstb"], w=["B3"], c=0.1)
            S.add("dve", lambda e: e.tensor_copy(
                out=hT[hT_slot][:, :, col0:col0 + rows],
                in_=B3h[:].rearrange("p (k t) -> p k t", k=8)[:, :, 0:rows]),
                r=["B3"], w=[f"hT{hT_slot}"], c=0.7)

        def inproj(j, hT_slot, T):
            S.newgrp()
            b_ = cnt["zb"] % 2
            cnt["zb"] += 1
            for k in range(8):
                S.add("pe", lambda e, k=k, b_=b_: e.matmul(B[b_][:, 0:T], win[:, k, j * 128:(j + 1) * 128],
                                                          hT[hT_slot][:, k, 0:T], start=(k == 0), stop=(k == 7)),
                      r=[f"{wg_of[j]}_{k // 4}", f"hT{hT_slot}"], w=[f"B{b_}"], c=CP(T))
            return B[b_], f"B{b_}"

        def qk_norm(bank, rb, T, wcol, out_ap, out_res, out_f32=None):
            sq_, rsq, lnb_, rln = (sq, "sq", lnb, "lnb") if T > NSMP else (sqs, "sqs", lnbs, "lnbs")
            _qk_norm(bank, rb, T, wcol, out_ap, out_res, out_f32, sq_, rsq, lnb_, rln)

        def _qk_norm(bank, rb, T, wcol, out_ap, out_res, out_f32, sq, rsq, lnb, rln):
            S.newgrp()
            S.add("act", lambda e: e.activation(out=sq[:, 0:T], in_=bank[:, 0:T], func=AF.Square), r=[rb], w=[rsq], c=CA(T))
            S.add("pe", lambda e: e.matmul(B[2][:, 0:T], blkb, sq[:, 0:T], start=True, stop=True),
                  r=[rsq, "cstb"], w=["B2"], c=CP(T))
            S.add("act", lambda e: e.activation(out=lnb[:, 0:T], in_=B[2][:, 0:T], func=AF.Ln, scale=1.0 / 64,
                                                bias=epsc), r=["B2", "cst"], w=[rln], c=CA(T), tbl=1)
            S.add("act", lambda e: e.activation(out=lnb[:, 0:T], in_=lnb[:, 0:T], func=AF.Exp, scale=-0.5),
                  r=[rln], w=[rln], c=CA(T), tbl=1)
            S.add("dve", lambda e: e.scalar_tensor_tensor(out=out_ap, in0=bank[:, 0:T], scalar=cst[:, wcol:wcol + 1],
                                                          in1=lnb[:, 0:T], op0=ALU.mult, op1=ALU.mult),
                  r=[rb, rln, "cst"], w=[out_res], c=CD(T))
            if out_f32 is not None:
                o_ap, o_res, c0, c1 = out_f32
                S.add("dve", lambda e: e.scalar_tensor_tensor(out=o_ap, in0=bank[:, c0:c1], scalar=cst[:, wcol:wcol + 1],
                                                              in1=lnb[:, c0:c1], op0=ALU.mult, op1=ALU.mult),
                      r=[rb, rln, "cst"], w=[o_res], c=0.25)

        def proj_k(hT_slot, T, kcol0, last_f32):
            rows = min(T, 128)
            bank, rb = inproj(0, hT_slot, T)
            if T == NSMP:
                qk_norm(bank, rb, T, C_KW, kTs[:, 0:T], "kTs", out_f32=(kTfs[:, 0:T], "kTfs", 0, T))
                return
            qk_norm(bank, rb, T, C_KW, kTn[:, kcol0:kcol0 + T], "kTn",
                    out_f32=(kTf[:, 0:rows], "kTf", T - rows, T) if last_f32 else None)

        def proj_q(c, hT_slot, T, qs):
            bank, rb = inproj(6 + c, hT_slot, T)
            qk_norm(bank, rb, T, C_QW, qTn[qs][:, c, 0:T], f"qTn{qs}")

        def proj_v(hT_slot, T, vblk0, last_f32):
            nblk = max(T // 128, 1)
            rows = min(T, 128)
            bank, rb = inproj(1, hT_slot, T)
            vT_, rvT = (vTf, "vTf") if T > NSMP else (vTfs, "vTfs")
            S.newgrp()
            S.add("act", lambda e: e.activation(out=vT_[:, 0:T], in_=bank[:, 0:T], func=AF.Copy), r=[rb], w=[rvT], c=CA(T))
            for bi in range(nblk):
                S.add("pe", lambda e, bi=bi: e.transpose(B[2][0:rows, bi * 128:(bi + 1) * 128],
                                                         vT_[:, bi * 128:bi * 128 + rows], identf),
                      r=[rvT, "cst"], w=["B2"], c=0.27)
            if T >= 128:
                S.add("dve", lambda e: e.tensor_copy(
                    out=Vall[:, vblk0:vblk0 + nblk, :],
                    in_=B[2][:, 0:nblk * 128].rearrange("p (b d) -> p b d", b=nblk)), r=["B2"], w=["Vall"], c=CD(T))
                if last_f32:
                    S.add("dve", lambda e: e.tensor_copy(out=vlf[:], in_=B[2][:, (nblk - 1) * 128:nblk * 128]),
                          r=["B2"], w=["vlf"], c=0.25)
            else:
                S.add("dve", lambda e: e.tensor_copy(out=vsf[:], in_=B[2][0:rows, 0:128]), r=["B2"], w=["vsf"], c=0.25)
                S.add("dve", lambda e: e.tensor_copy(out=vsb[:], in_=B[2][0:rows, 0:128]), r=["B2"], w=["vsb"], c=0.25)

        def proj_u(g, hT_slot, T, ucol0, halo=False):
            bank, rb = inproj(2 + g, hT_slot, T)
            if T == NSMP:
                S.add("act", lambda e: e.activation(out=uTs[:, g, 0:T], in_=bank[:, 0:T], func=AF.Copy),
                      r=[rb], w=[f"uTs{g}"], c=CA(T))
            elif not halo:
                S.add("act", lambda e: e.activation(out=uT[:, g, ucol0:ucol0 + T], in_=bank[:, 0:T], func=AF.Copy),
                      r=[rb], w=[f"uT{g}"], c=CA(T))
            else:
                S.add("act", lambda e: e.activation(out=uT[:, g, 0:16], in_=bank[:, T - 16:T], func=AF.Copy),
                      r=[rb], w=["uTh"], c=0.3)

        def proj_ga(c, hT_slot, T):
            bank, rb = inproj(10 + c, hT_slot, T)
            dst, rd = (sga, "sga") if T > NSMP else (sgas, "sgas")
            S.add("act", lambda e: e.activation(out=dst[:, c, 0:T], in_=bank[:, 0:T], func=AF.Silu), r=[rb], w=[rd], c=CA(T), tbl=2)

        def proj_gp(g, hT_slot, T):
            bank, rb = inproj(14 + g, hT_slot, T)
            dst, rd = (sgp, "sgp") if T > NSMP else (sgps, "sgps")
            S.add("act", lambda e: e.activation(out=dst[:, g, 0:T], in_=bank[:, 0:T], func=AF.Silu), r=[rb], w=[rd], c=CA(T), tbl=2)

        def attn_scores(bi, cp, kcol, first, qs):
            S.newgrp()
            q0 = bi * 128
            for h in range(2):
                hp = slice(64 * h, 64 * h + 64)
                sb_ = B[4 + h]
                for pc in range(2):
                    kc = kcol - 128 + 128 * pc
                    S.add("pe", lambda e, hp=hp, sb_=sb_, pc=pc, kc=kc, h=h: e.matmul(
                        sb_[:, pc * 256:(pc + 1) * 256].rearrange("p (c q) -> p c q", c=2),
                        kTn[hp, kc:kc + 128], qTn[qs][hp, 2 * cp:2 * cp + 2, q0:q0 + 128],
                        start=True, stop=True, tile_position=(64 * h, 0)),
                        r=["kTn", f"qTn{qs}"], w=[f"B{4 + h}"], c=0.15, pm=1)
                pt = PT[2 * cp + h]
                rpt = f"PT{2 * cp + h}"
                S.add("act", lambda e, sb_=sb_, pt=pt: e.activation(
                    out=pt[:].rearrange("p a c q -> p (a c q)"), in_=sb_[:], func=AF.Exp, scale=0.125),
                    r=[f"B{4 + h}"], w=[rpt], c=0.55, tbl=1)
                if first:
                    S.add("dve", lambda e, pt=pt: e.tensor_tensor(
                        out=pt[:, 0], in0=pt[:, 0], in1=mask_f.unsqueeze(1).to_broadcast([128, 2, 128]),
                        op=ALU.mult), r=[rpt, "cstb"], w=[rpt], c=0.3)
                    S.add("dve", lambda e, pt=pt: e.tensor_tensor(
                        out=pt[:, 1], in0=pt[:, 1], in1=mask_c.unsqueeze(1).to_broadcast([128, 2, 128]),
                        op=ALU.mult), r=[rpt, "cstb"], w=[rpt], c=0.3)
                else:
                    S.add("dve", lambda e, pt=pt: e.tensor_tensor(
                        out=pt[:], in0=pt[:], in1=mask_pc.unsqueeze(2).to_broadcast([128, 2, 2, 128]),
                        op=ALU.mult), r=[rpt, "cstb"], w=[rpt], c=0.45)

        def attn_pv(bi, cp, vblk):
            S.newgrp()
            for h in range(2):
                hp = slice(64 * h, 64 * h + 64)
                pt = PT[2 * cp + h]
                rpt = f"PT{2 * cp + h}"
                for pc in range(2):
                    S.add("pe", lambda e, hp=hp, pt=pt, pc=pc, h=h: e.matmul(
                        B[7][hp, cp * 256:(cp + 1) * 256].rearrange("p (c q) -> p c q", c=2),
                        onesb[:, 0:64], pt[:, pc], start=(pc == 0), stop=(pc == 1),
                        tile_position=(0, 64 * h)), r=[rpt, "onesb"], w=["B7"], c=0.12, pm=2)
                for ci in range(2):
                    c = 2 * cp + ci
                    for pc in range(2):
                        vb = vblk - 1 + pc
                        S.add("pe", lambda e, hp=hp, pt=pt, pc=pc, ci=ci, c=c, vb=vb, h=h: e.matmul(
                            B[6][hp, c * 128:(c + 1) * 128], Vall[:, vb, 64 * h:64 * h + 64], pt[:, pc, ci, :],
                            start=(pc == 0), stop=(pc == 1), tile_position=(0, 64 * h)),
                            r=[rpt, "Vall"], w=["B6"], c=0.06, pm=2)

        def attn_epi(bi):
            S.newgrp()
            q0 = bi * 128
            S.add("dve", lambda e: e.tensor_tensor(
                out=rden[:].rearrange("p (c q) -> p c q", c=4), in0=B[7][:].rearrange("p (c q) -> p c q", c=4),
                in1=esink[:].unsqueeze(2).to_broadcast([128, 4, 128]), op=ALU.add), r=["B7", "esink"], w=["rden"], c=0.65)
            S.add("act", lambda e: e.activation(out=rden[:], in_=rden[:], func=AF.Ln), r=["rden"], w=["rden"], c=0.6, tbl=1)
            S.add("act", lambda e: e.activation(out=rden[:], in_=rden[:], func=AF.Exp, scale=-1.0), r=["rden"], w=["rden"], c=0.6, tbl=1)
            S.add("dve", lambda e: e.tensor_tensor(out=oT[:], in0=B[6][:], in1=rden[:], op=ALU.mult),
                  r=["B6", "rden"], w=["oT"], c=0.65)
            S.add("pool", lambda e: e.tensor_tensor(
                out=mixT[:, 0:4, q0:q0 + 128], in0=oT[:].rearrange("p (c q) -> p c q", c=4),
                in1=sga[:, :, q0:q0 + 128], op=ALU.mult), r=["oT", "sga"], w=[f"mixTa{bi}"], c=1.3)

        def pool_group(g, T, first_tile, bank_i):
            w_ = POOLW[g]
            U = uT[:, g, :]
            src, src_res = U, None
            sh, bi, lo = 1, 0, 0
            bufs = [pta, ptb]
            while sh < w_:
                dst = bufs[bi]
                dres = "pta" if bi == 0 else "ptb"
                lo2 = lo + sh
                rr = [f"uT{g}", "uTh"] if src_res is None else [src_res]
                S.add("pool", lambda e, dst=dst, src=src, lo2=lo2, sh=sh: e.tensor_tensor(
                    out=dst[:, lo2:16 + T], in0=src[:, lo2:16 + T], in1=src[:, lo2 - sh:16 + T - sh], op=ALU.add),
                    r=rr, w=[dres], c=1.1)
                src, src_res = dst, dres
                lo = lo2
                sh *= 2
                bi ^= 1
            S.add("dve", lambda e, src=src: e.scalar_tensor_tensor(
                out=dT[:, g, 0:T], in0=src[:, 16:16 + T], scalar=1.0 / w_, in1=uT[:, g, 16:16 + T],
                op0=ALU.mult, op1=ALU.subtract), r=[src_res, f"uT{g}"], w=["dT"], c=CD(T))
            if first_tile:
                S.add("dve", lambda e, src=src: e.tensor_tensor(
                    out=dfix[:], in0=src[:, 16:32], in1=cst[:, C_RC + 16 * g:C_RC + 16 * g + 16], op=ALU.mult),
                    r=[src_res, "cst"], w=["dfix"], c=0.15)
                S.add("dve", lambda e: e.tensor_tensor(
                    out=dT[:, g, 0:16], in0=dfix[:], in1=uT[:, g, 16:32], op=ALU.subtract),
                    r=["dfix", f"uT{g}"], w=["dT"], c=0.15)
            pool_proj(g, T, bank_i)

        def pool_proj(g, T, bank_i):
            S.newgrp()
            dT_, rdT, mx, rmx, sg, rsg = ((dT, "dT", mixT, "mixTp", sgp, "sgp") if T > NSMP
                                          else (dTs, "dTs", mixTs, "mixTs", sgps, "sgps"))
            S.add("pe", lambda e: e.matmul(B[bank_i][:, 0:T], wpool[:, g, :], dT_[:, g, 0:T], start=True, stop=True),
                  r=["wpool", rdT], w=[f"B{bank_i}"], c=CP(T))
            S.add("dve", lambda e: e.scalar_tensor_tensor(
                out=mx[:, 4 + g, 0:T], in0=B[bank_i][:, 0:T], scalar=cst[:, C_PS + g:C_PS + g + 1],
                in1=sg[:, g, 0:T], op0=ALU.mult, op1=ALU.mult), r=[f"B{bank_i}", "cst", rsg], w=[rmx], c=CD(T))

        def out_block(col0, rows, x_src, y_dst, banks):
            S.newgrp()
            smp = rows == NSMP
            mx = mixTs if smp else mixT
            for nh in range(2):
                bk = banks[nh]
                for m in range(8):
                    rmx = "mixTs" if smp else (f"mixTa{col0 // 128}" if m < 4 else "mixTp")
                    S.add("pe", lambda e, nh=nh, m=m, bk=bk: e.matmul(
                        B[bk][0:rows, :], mx[:, m, col0:col0 + rows], wout[:, m, nh * 512:(nh + 1) * 512],
                        start=(m == 0), stop=(m == 7)),
                        r=[rmx, f"wout{m}"], w=[f"B{bk}"], c=0.225)
            rs_ = cnt["xr"] % 2
            cnt["xr"] += 1
            ys_ = cnt["ysb"] % 2
            cnt["ysb"] += 1
            XR, Y = xr[rs_], ysb[ys_]
            S.add("sp", lambda e: e.dma_start(out=XR[0:rows, :], in_=x_src), w=[f"xr{rs_}"], dma=f"xr{rs_}", nb=rows * DM * 4)
            for nh in range(2):
                bk = banks[nh]
                S.add("dve", lambda e, nh=nh, bk=bk: e.tensor_tensor(
                    out=Y[0:rows, nh * 512:(nh + 1) * 512], in0=B[bk][0:rows, :],
                    in1=XR[0:rows, nh * 512:(nh + 1) * 512], op=ALU.add),
                    r=[f"B{bk}", f"xr{rs_}"], w=[f"ysb{ys_}"], c=0.65)
            fr = fence(Y[0:rows, DM - 1:DM], rows, f"ysb{ys_}", ys_)
            S.add("pool" if rows == 128 else "sp", lambda e: e.dma_start(out=y_dst, in_=Y[0:rows, :]),
                  r=[f"ysb{ys_}", fr], dma=f"ysb{ys_}", nb=rows * DM * 4)

        def B_steps(ti):
            hs_ = (ti + 1) % 2
            qs = ti % 2
            t0 = ti * TT
            last = ti == NTILE - 1
            X = []
            for bi in range(4):
                X.append(lambda bi=bi: x_stage(xh[128 + t0 + bi * 128:128 + t0 + (bi + 1) * 128, :], 128, hs_, bi * 128))
            G1 = []
            G1.append(lambda: proj_k(hs_, TT, 128 + t0, last))
            G1.append(lambda: proj_v(hs_, TT, 1 + ti * 4, last))
            for c in range(2):
                G1.append(lambda c=c: proj_q(c, hs_, TT, qs))
            if last:
                def _kv_out():
                    f1 = fence(kTf[:, 127:128], 128, "kTf", 2)
                    f2 = fence(vlf[:, 127:128], 128, "vlf", 3)
                    S.add("sp", lambda e: e.dma_start(out=kl_d, in_=kTf[:]), r=["kTf", f1], dma="kl")
                    S.add("sp", lambda e: e.dma_start(out=vl_d, in_=vlf[:]), r=["vlf", f2], dma="vl")
                G1.append(_kv_out)
            G2 = [lambda c=c: proj_q(c, hs_, TT, qs) for c in range(2, 4)]
            G2 += [lambda c=c: proj_ga(c, hs_, TT) for c in range(4)]
            G3 = []
            if ti > 0:
                G3.append(lambda: S.add("pool", lambda e: e.tensor_copy(out=uT[:, :, 0:16], in_=uT[:, :, TT:TT + 16]),
                                        r=[f"uT{g}" for g in range(4)], w=["uTh"]))
            for g in range(4):
                G3.append(lambda g=g: proj_u(g, hs_, TT, 16))
            for g in range(4):
                G3.append(lambda g=g: proj_gp(g, hs_, TT))
            return X, G1, G2, G3

        def A_steps(ti):
            qs = ti % 2
            t0 = ti * TT
            last = ti == NTILE - 1
            sc = []
            for s in range(8):
                bi, cp = s // 2, s % 2
                sc.append((bi, cp))
            P1 = []
            def SC(s):
                bi, cp = sc[s]
                return lambda: attn_scores(bi, cp, 128 + t0 + bi * 128, ti == 0 and bi == 0, qs)
            def DO(s):
                bi, cp = sc[s]
                return lambda: attn_pv(bi, cp, 1 + ti * 4 + bi)
            P1.append(SC(0))
            for s in range(8):
                if s + 1 < 8:
                    P1.append(SC(s + 1))
                P1.append(DO(s))
                if sc[s][1] == 1:
                    P1.append(lambda bi=sc[s][0]: attn_epi(bi))
                    if last:
                        bi = sc[s][0]
                        r0 = t0 + bi * 128
                        P1.append(lambda bi=bi, r0=r0: out_block(bi * 128, 128, xh[128 + r0:128 + r0 + 128, :],
                                                                 y_d[r0:r0 + 128, :], (0, 1) if bi % 2 == 0 else (2, 3)))
            P2 = [lambda g=g: pool_group(g, TT, ti == 0, (2 + (g % 2)) if last else (4 + (g % 2))) for g in range(4)]
            P3 = []
            for bi in range(4):
                r0 = t0 + bi * 128
                if not last:
                    P3.append(lambda bi=bi, r0=r0: out_block(bi * 128, 128, xh[128 + r0:128 + r0 + 128, :],
                                                             y_d[r0:r0 + 128, :], (6, 7) if bi % 2 == 0 else (4, 5)))
            if last:
                def _u_out():
                    for g in range(4):
                        S.add("pe", lambda e, g=g: e.transpose(B[2][0:16, g * 128:(g + 1) * 128],
                                                               uT[:, g, TT:TT + 16], identf),
                              r=[f"uT{g}", "cst"], w=["B2"])
                    S.add("dve", lambda e: e.tensor_copy(out=usf[:], in_=B[2][0:16, :]), r=["B2"], w=["usf"])
                    f3 = fence(usf[:, 511:512], 16, "usf", 4)
                    S.add("sp", lambda e: e.dma_start(out=ul_d, in_=usf[:]), r=["usf", f3], dma="ul")
                P2.append(_u_out)
            return P1, P2, P3

        def run(steps):
            for f in steps:
                S.newgrp()
                f()

        x_stage(xh[0:128, :], 128, 0, 0)
        load_w_in()
        proj_k(0, 128, 0, False)
        proj_v(0, 128, 0, False)
        for g in range(4):
            proj_u(g, 0, 128, 0, halo=True)
        BS = [B_steps(ti) for ti in range(NTILE)]
        X, G1, G2, G3 = BS[0]
        run(X)
        run(G1)
        load_late_weights()
        run(G2)
        run(_merge(G3, BS[1][0]))
        for ti in range(NTILE):
            P1, P2, P3 = A_steps(ti)
            if ti + 1 < NTILE:
                _, G1, G2, G3 = BS[ti + 1]
                if ti + 2 < NTILE:
                    G3 = G3 + BS[ti + 2][0]
                if ti + 2 == NTILE:
                    G3 = G3 + sample_steps(nc, S, locals())
                run(_merge(P1, G1))
                run(_merge(P2, G2))
                run(_merge(P3, G3))
            else:
                run(P2)
                run(P1)
                run(P3)

        S.emit()
        _CACHE["sched"] = S
    return nc


def sample_steps(nc, S, L):
    (xs, ck, cv, sp_in, ys_d, nks_d, nvs_d, nps_d) = (L[k] for k in
                                                    ("xs", "ck", "cv", "sp_in", "ys_d", "nks_d", "nvs_d", "nps_d"))
    B, cst, cstb, onesb, esink = L["B"], L["cst"], L["cstb"], L["onesb"], L["esink"]
    identf, identb, B3h = L["identf"], L["identb"], L["B3h"]
    ckb, cvb, KT, spf, uh, hs = L["ckb"], L["cvb"], L["KT"], L["spf"], L["uh"], L["hs"]
    PTs, pnf, pnb, vsf, vsb, ksf, usf = L["PTs"], L["pnf"], L["pnb"], L["vsf"], L["vsb"], L["ksf"], L["usf"]
    rdens, oTs = L["rdens"], L["oTs"]
    kTs, kTfs, uTs, dTs, mixTs, sgas = L["kTs"], L["kTfs"], L["uTs"], L["dTs"], L["mixTs"], L["sgas"]
    qTn = L["qTn"][2]
    N = NSMP
    SM = {"pe": 0.07, "act": 0.35, "dve": 0.25, "pool": 0.5, "sp": 3.0}

    def add(eng, fn, r=(), w=(), dma=None, c=None):
        if c is None:
            c = SM["sp"] if dma is not None else SM[eng]
        S.add(eng, fn, r=r, w=w, dma=dma, c=c, tbl=1 if eng == "act" else 0)

    steps = []

    def s_copies():
        add("sp", lambda e: e.dma_start(out=nks_d[:, 0:127, :], in_=ck[:, 1:128, :]), r=["ckb"], dma="nks_c")
        add("sp", lambda e: e.dma_start(out=nvs_d[:, 0:127, :], in_=cv[:, 1:128, :]), r=["cvb"], dma="nvs_c")
        add("sp", lambda e: e.dma_start(out=nps_d[:, 0:14, :], in_=sp_in[:, 1:15, :]), r=["cvb"], dma="nps_c")
        add("sp", lambda e: e.dma_start(out=spf[:], in_=sp_in.rearrange("(a b) t c -> (b t) a c", a=2)),
            r=["cvb"], w=["spf"], dma="spf")
    steps.append(s_copies)

    def s_kt(q8):
        def f():
            for i in range(8):
                b = q8 * 8 + i
                add("pe", lambda e, b=b, i=i: e.transpose(B3h[:, i * 128:(i + 1) * 128], ckb[:, b, :], identb),
                    r=["ckb", "cstb"], w=["B3"], c=0.08)
            add("dve", lambda e: e.tensor_copy(out=KT[:, q8 * 8:(q8 + 1) * 8, :],
                                               in_=B3h[:].rearrange("p (b j) -> p b j", b=8)),
                r=["B3"], w=["KT"], c=0.7)
        return f
    steps.append(s_kt(0))
    steps.append(s_kt(1))

    def s_hist(a):
        def f():
            for g in range(4):
                add("pe", lambda e, g=g: e.transpose(B[a][:, g * 128:g * 128 + 120],
                                                     spf[:, a, g * 128:(g + 1) * 128], identf[0:120, 0:120]),
                    r=["spf", "cst"], w=[f"B{a}"], c=0.27)
            add("dve", lambda e: e.tensor_copy(
                out=uh[:, :, a, :], in_=B[a][:].rearrange("p (g t) -> p g t", g=4)[:, :, 0:120]),
                r=[f"B{a}"], w=["uh"], c=0.6)
        return f
    steps.append(s_hist(0))
    steps.append(s_hist(1))

    steps.append(lambda: L["x_stage"](xs, N, 2, 0))
    steps.append(lambda: L["proj_k"](2, N, 0, True))
    for c in range(4):
        steps.append(lambda c=c: L["proj_q"](c, 2, N, 2))
    steps.append(lambda: L["proj_v"](2, N, 0, False))
    for g in range(4):
        steps.append(lambda g=g: L["proj_u"](g, 2, N, 0))
    for c in range(4):
        steps.append(lambda c=c: L["proj_ga"](c, 2, N))
    for g in range(4):
        steps.append(lambda g=g: L["proj_gp"](g, 2, N))

    def s_newrows():
        add("pe", lambda e: e.transpose(B[2][0:N, 0:128], kTfs[:, 0:N], identf), r=["kTfs", "cst"], w=["B2"], c=0.27)
        add("dve", lambda e: e.tensor_copy(out=ksf[:], in_=B[2][0:N, 0:128]), r=["B2"], w=["ksf"])
        f5 = L["fence"](ksf[:, 127:128], N, "ksf", 5)
        f6 = L["fence"](vsf[:, 127:128], N, "vsf", 6)
        add("sp", lambda e: e.dma_start(out=nks_d[:, 127, :], in_=ksf[:]), r=["ksf", f5], dma="nks_n")
        add("sp", lambda e: e.dma_start(out=nvs_d[:, 127, :], in_=vsf[:]), r=["vsf", f6], dma="nvs_n")
        for g in range(4):
            add("pe", lambda e, g=g: e.transpose(B[2][0:N, g * 128:(g + 1) * 128], uTs[:, g, 0:N], identf),
                r=[f"uTs{g}", "cst"], w=["B2"], c=0.27)
        add("dve", lambda e: e.tensor_copy(out=usf[:], in_=B[2][0:N, :]), r=["B2"], w=["usf"])
        f7 = L["fence"](usf[:, 511:512], N, "usf", 7)
        add("sp", lambda e: e.dma_start(out=nps_d[:, 14, :], in_=usf[:]), r=["usf", f7], dma="nps_n")
    steps.append(s_newrows)

    def s_scores(h):
        def f():
            hp = slice(64 * h, 64 * h + 64)
            bank = B[h]
            rb = f"B{h}"
            for b in range(N):
                add("pe", lambda e, b=b: e.matmul(
                    bank[:, b * 4:(b + 1) * 4], KT[hp, b, :], qTn[hp, :, b], start=True, stop=True,
                    tile_position=(64 * h, 0)), r=["KT", "qTn2"], w=[rb])
            add("pe", lambda e: e.matmul(
                bank[0:N, 64:128].rearrange("p (c b) -> p c b", c=4), kTs[hp, 0:N], qTn[hp, :, 0:N],
                start=True, stop=True, tile_position=(64 * h, 0)), r=["kTs", "qTn2"], w=[rb])
            add("act", lambda e: e.activation(
                out=PTs[:, h].rearrange("p b c -> p (b c)"), in_=bank[:, 0:64], func=AF.Exp, scale=0.125),
                r=[rb], w=[f"PTs{h}"])
            add("act", lambda e: e.activation(
                out=pnf[:, h].rearrange("p c b -> p (c b)"), in_=bank[0:N, 64:128], func=AF.Exp, scale=0.125),
                r=[rb], w=[f"pnf{h}"])
            add("dve", lambda e: e.tensor_tensor(
                out=pnb[:, h].rearrange("p c b -> p (c b)"), in0=pnf[:, h].rearrange("p c b -> p (c b)"),
                in1=cst[0:N, C_DM:C_DM + 64], op=ALU.mult), r=[f"pnf{h}", "cst"], w=[f"pnb{h}"])
        return f
    steps.append(s_scores(0))
    steps.append(s_scores(1))

    def s_pv(h):
        def f():
            hp = slice(64 * h, 64 * h + 64)
            add("pe", lambda e: e.matmul(
                B[2][hp, 0:64].rearrange("p (b c) -> p b c", b=N), onesb[0:N, 0:64],
                pnb[:, h].rearrange("p c b -> p b c"), start=True, stop=False, tile_position=(0, 64 * h),
                skip_group_check=True), r=[f"pnb{h}", "onesb"], w=["B2"])
            add("pe", lambda e: e.matmul(
                B[2][hp, 0:64], onesb[:, 0:64], PTs[:, h].rearrange("p b c -> p (b c)"),
                start=False, stop=True, tile_position=(0, 64 * h), skip_group_check=True),
                r=[f"PTs{h}", "onesb"], w=["B2"])
            add("pe", lambda e: e.matmul(
                B[3][hp, 0:64].rearrange("p (b c) -> p b c", b=N), vsb[:, 64 * h:64 * h + 64],
                pnb[:, h].rearrange("p c b -> p b c"), start=True, stop=False, tile_position=(0, 64 * h),
                skip_group_check=True), r=[f"pnb{h}", "vsb"], w=["B3"])
            for b in range(N):
                add("pe", lambda e, b=b: e.matmul(
                    B[3][hp, b * 4:(b + 1) * 4], cvb[:, b, 64 * h:64 * h + 64], PTs[:, h, b, :],
                    start=False, stop=(b == N - 1), tile_position=(0, 64 * h), skip_group_check=True),
                    r=[f"PTs{h}", "cvb"], w=["B3"])
        return f

    def s_epi():
        add("dve", lambda e: e.tensor_tensor(
            out=rdens[:], in0=B[2][:, 0:64].rearrange("p (b c) -> p b c", b=N),
            in1=esink[:].unsqueeze(1).to_broadcast([128, N, 4]), op=ALU.add), r=["B2", "esink"], w=["rdens"])
        add("dve", lambda e: e.reciprocal(out=rdens[:], in_=rdens[:]), r=["rdens"], w=["rdens"], c=0.5)
        add("dve", lambda e: e.tensor_tensor(out=oTs[:], in0=B[3][:, 0:64].rearrange("p (b c) -> p b c", b=N),
                                             in1=rdens[:], op=ALU.mult), r=["B3", "rdens"], w=["oTs"])
        add("dve", lambda e: e.tensor_tensor(out=mixTs[:, 0:4, 0:N], in0=oTs[:].rearrange("p b c -> p c b"),
                                             in1=sgas[:, :, 0:N], op=ALU.mult), r=["oTs", "sgas"], w=["mixTs"])
    def s_pv_epi():
        s_pv(0)()
        s_pv(1)()
        s_epi()
    steps.append(s_pv_epi)

    def s_pool(g):
        def f():
            w_ = POOLW[g]
            add("dve", lambda e: e.tensor_reduce(
                out=hs[:, g, :], in_=uh[:, g].rearrange("p a (b t) -> p (a b) t", t=15)[:, :, 16 - w_:15],
                axis=AX.X, op=ALU.add), r=["uh"], w=[f"hs{g}"])
            add("dve", lambda e: e.tensor_tensor(out=hs[:, g, :], in0=hs[:, g, :], in1=uTs[:, g, 0:N],
                                                 op=ALU.add), r=[f"hs{g}", f"uTs{g}"], w=[f"hs{g}"])
            add("dve", lambda e: e.scalar_tensor_tensor(
                out=dTs[:, g, 0:N], in0=hs[:, g, :], scalar=1.0 / w_, in1=uTs[:, g, 0:N],
                op0=ALU.mult, op1=ALU.subtract), r=[f"hs{g}", f"uTs{g}"], w=["dTs"])
            L["pool_proj"](g, N, g % 2)
        return f
    for g in range(4):
        steps.append(s_pool(g))
    steps.append(lambda: L["out_block"](0, N, xs, ys_d, (0, 1)))
    return steps


def _col_perm():
    p = list(range(512, 768))
    p += list(range(1280, 1792))
    for c in range(4):
        p += list(range(c * 64, c * 64 + 64)) + list(range((c + 4) * 64, (c + 4) * 64 + 64))
    for c in range(4):
        p += list(range(768 + c * 64, 768 + c * 64 + 64)) + list(range(768 + (c + 4) * 64, 768 + (c + 4) * 64 + 64))
    p += list(range(1792, 2304))
    return np.array(p)


def _row_perm():
    p = []
    for c in range(4):
        p += list(range(c * 64, c * 64 + 64)) + list(range((c + 4) * 64, (c + 4) * 64 + 64))
    p += list(range(512, 1024))
    return np.array(p)


def _consts(core, norm_w, q_norm_w, k_norm_w, sinks, pool_scale):
    c = np.zeros((128, NCST), np.float32)
    j = np.arange(128)[:, None]
    i = np.arange(128)[None, :]
    c[:, C_ID:C_ID + 128] = np.eye(128, dtype=np.float32)
    c[:, C_MC:C_MC + 128] = (j <= i)
    c[:, C_MP:C_MP + 128] = (j >= i)
    c[:, C_MF:C_MF + 128] = 0.0 if core == 0 else (j >= i)
    c[:, C_BLK:C_BLK + 128] = ((j // 64) == (i // 64))
    c[:, C_QW] = np.tile(q_norm_w, 2)
    c[:, C_KW] = np.tile(k_norm_w, 2)
    c[:, C_NW:C_NW + 8] = norm_w.reshape(8, 128).T
    c[:, C_PS:C_PS + 4] = pool_scale.reshape(4, 128).T
    for cc in range(4):
        c[0:64, C_SK + cc] = sinks[cc]
        c[64:128, C_SK + cc] = sinks[cc + 4]
    for g, w in enumerate(POOLW):
        pos = np.arange(16)
        cntv = np.minimum(w, pos + 1) if core == 0 else np.full(16, w)
        c[:, C_RC + 16 * g:C_RC + 16 * g + 16] = (1.0 / cntv)[None, :]
    dm = np.zeros((16, 4, 16), np.float32)
    for b in range(16):
        dm[b, :, b] = 1.0
    c[0:16, C_DM:C_DM + 64] = dm.reshape(16, 64)
    c[:, C_EPS] = EPS
    c[:, C_M1] = -1.0
    return c


def kernel(x_prompt, x_sample, cache_k, cache_v, state_pool, norm_w, w_in, q_norm_w, k_norm_w, sinks,
           w_pool, pool_scale, w_out):
    f = lambda a: np.ascontiguousarray(np.asarray(a, dtype=np.float32))
    xp = f(x_prompt)[0]
    xsm = f(x_sample)[:, 0, :]
    ckh = f(cache_k)[0].reshape(128, 128, 128)
    cvh = f(cache_v)[0].reshape(128, 128, 128)
    sph = f(state_pool)[0]
    w_in_p = np.ascontiguousarray(f(w_in)[0][:, _col_perm()])
    w_out_p = np.ascontiguousarray(f(w_out)[0][_row_perm(), :])
    w_pool_h = f(w_pool)[0]
    if "nc" not in _CACHE:
        _CACHE["nc"] = build_program()
    nc = _CACHE["nc"]
    nwb_h = np.ascontiguousarray(np.tile(f(norm_w)[0][None, :], (128, 1)))
    in_maps = []
    for c in range(NCORE):
        xhc = np.zeros((TOK + 128, DM), np.float32)
        lo = c * TOK
        if c > 0:
            xhc[0:128] = xp[lo - 128:lo]
        xhc[128:] = xp[lo:lo + TOK]
        in_maps.append({
            "xh": xhc,
            "xs": np.ascontiguousarray(xsm[c * NSMP:(c + 1) * NSMP]),
            "ck": np.ascontiguousarray(ckh[c * NSMP:(c + 1) * NSMP]),
            "cv": np.ascontiguousarray(cvh[c * NSMP:(c + 1) * NSMP]),
            "sp": np.ascontiguousarray(sph[c * NSMP:(c + 1) * NSMP]),
            "w_in": w_in_p, "w_out": w_out_p, "w_pool": w_pool_h,
            "cst": _consts(c, f(norm_w)[0], f(q_norm_w)[0], f(k_norm_w)[0], f(sinks)[0], f(pool_scale)[0]),
            "nwb": nwb_h,
        })
    res = run_bass_kernel_spmd(nc, in_maps, core_ids=list(range(NCORE)))
    R = res.results
    y = np.concatenate([R[c]["y"] for c in range(NCORE)], axis=0)[None]
    ys = np.concatenate([R[c]["ys"] for c in range(NCORE)], axis=0)[:, None, :]
    klast = np.ascontiguousarray(R[NCORE - 1]["klast"].T).reshape(1, 1, 128, 2, 64)
    vlast = R[NCORE - 1]["vlast"].reshape(1, 1, 128, 2, 64)
    ulast = R[NCORE - 1]["ulast"][1:16].reshape(1, 1, 15, 512)
    nks = np.concatenate([R[c]["nks"] for c in range(NCORE)], axis=0).reshape(1, 128, 128, 2, 64)
    nvs = np.concatenate([R[c]["nvs"] for c in range(NCORE)], axis=0).reshape(1, 128, 128, 2, 64)
    nps = np.concatenate([R[c]["nps"] for c in range(NCORE)], axis=0).reshape(1, 128, 15, 512)
    return (y.astype(np.float32), ys.astype(np.float32), klast.astype(np.float32), vlast.astype(np.float32),
            ulast.astype(np.float32), nks.astype(np.float32), nvs.astype(np.float32), nps.astype(np.float32))
```
